# Optimizing a Trainium2 kernel written in Bass

```python
import math
import jax, jax.numpy as jnp
from jax import lax
import numpy as np

D_MODEL = 2048
BATCH = 2
SEQ = 4096
DEPTH = 1

D_MIX = D_MODEL
EPS = 1e-6
M_WIDTH = D_MIX // 2
M_HEADS = 4
M_HD = M_WIDTH // M_HEADS
M_CHUNK = 64
CONV_K = 4
N_WIDTH = D_MIX - M_WIDTH
N_HEADS = 16
N_HD = N_WIDTH // N_HEADS
N_KV = 4
N_HPG = N_HEADS // N_KV
KV_W = N_KV * N_HD
CMP_LEN = 32
CMP_STRIDE = 16
CMP_HIDDEN = 2 * N_HD
SLC_LEN = 64
SLC_TOPN = 16
WIN = 512
QBLK = 128

IN_SPLITS = (M_WIDTH, M_WIDTH, M_WIDTH, M_WIDTH, M_HEADS, M_HEADS,
             N_WIDTH, KV_W, KV_W, KV_W, KV_W, KV_W, KV_W, 3 * N_HEADS, N_WIDTH)
D_IN = sum(IN_SPLITS)

kernel_name = "hymba_mlstm_nsa_adaln_layer"


def rmsnorm(x, g):
    x = x.astype(jnp.float32)
    return x * lax.rsqrt(jnp.mean(x * x, axis=-1, keepdims=True) + EPS) * g


def masked_softmax(s, mask, axis):
    s = jnp.where(mask, s.astype(jnp.float32), -jnp.inf)
    m = jnp.max(s, axis=axis, keepdims=True)
    m = jnp.where(jnp.isfinite(m), m, 0.0)
    e = jnp.where(mask, jnp.exp(s - m), 0.0)
    z = jnp.sum(e, axis=axis, keepdims=True)
    return e / jnp.where(z > 0, z, 1.0)


def alibi_slopes():
    return np.array([2.0 ** (-8.0 * (h + 1) / N_HEADS) for h in range(N_HEADS)], np.float32)


def mlstm_chunkwise(q, k, v, i_pre, f_pre):
    B, H, T, D = q.shape
    L = M_CHUNK
    nch = T // L
    k = k * (D ** -0.5)
    logf = jax.nn.log_sigmoid(f_pre)
    to_c = lambda a: jnp.moveaxis(a.reshape(a.shape[:2] + (nch, L) + a.shape[3:]), 2, 0)
    xs = (to_c(q), to_c(k), to_c(v), to_c(i_pre), to_c(logf))
    causal = jnp.tril(jnp.ones((L, L), bool))

    def step(carry, inp):
        C, n, m = carry
        qb, kb, vb, ib, fb = inp
        b = jnp.cumsum(fb, axis=-1)
        Dm = jnp.where(causal, b[..., :, None] - b[..., None, :] + ib[..., None, :], -jnp.inf)
        inter = b + m[..., None]
        m_t = jnp.maximum(inter, jnp.max(Dm, axis=-1))
        w_in = jnp.exp(Dm - m_t[..., None])
        w_st = jnp.exp(inter - m_t)
        s = jnp.einsum('bhld,bhsd->bhls', qb, kb) * w_in
        num = w_st[..., None] * jnp.einsum('bhld,bhde->bhle', qb, C) + jnp.einsum('bhls,bhse->bhle', s, vb)
        den = w_st * jnp.einsum('bhld,bhd->bhl', qb, n) + jnp.sum(s, axis=-1)
        h = num / jnp.maximum(jnp.abs(den), jnp.exp(-m_t))[..., None]
        bL = b[..., -1]
        w_end = bL[..., None] - b + ib
        m_new = jnp.maximum(bL + m, jnp.max(w_end, axis=-1))
        decay = jnp.exp(bL + m - m_new)
        wk = jnp.exp(w_end - m_new[..., None])
        C_new = decay[..., None, None] * C + jnp.einsum('bhs,bhsd,bhse->bhde', wk, kb, vb)
        n_new = decay[..., None] * n + jnp.einsum('bhs,bhsd->bhd', wk, kb)
        return (C_new, n_new, m_new), h

    init = (jnp.zeros((B, H, D, D), jnp.float32), jnp.zeros((B, H, D), jnp.float32),
            jnp.zeros((B, H), jnp.float32))
    _, hs = lax.scan(step, init, xs)
    return jnp.moveaxis(hs, 0, 2).reshape(B, H, T, D)


def mlstm_group(xm, vm, om, zm, im, fm, conv_w, conv_b, wq, wk, norm_w, skip, f_bias):
    B, T, _ = xm.shape
    xpad = jnp.pad(xm, ((0, 0), (CONV_K - 1, 0), (0, 0)))
    xc = conv_b + xpad[:, 0:T] * conv_w[0]
    for j in range(1, CONV_K):
        xc = xc + xpad[:, j:j + T] * conv_w[j]
    xc = jax.nn.silu(xc)
    xch = xc.reshape(B, T, M_HEADS, M_HD)
    q = jnp.einsum('bthd,hde->bhte', xch, wq)
    k = jnp.einsum('bthd,hde->bhte', xch, wk)
    v = vm.reshape(B, T, M_HEADS, M_HD).transpose(0, 2, 1, 3)
    h = mlstm_chunkwise(q, k, v, im.transpose(0, 2, 1), (fm + f_bias).transpose(0, 2, 1))
    h = h.transpose(0, 2, 1, 3)
    mu = jnp.mean(h, axis=-1, keepdims=True)
    var = jnp.mean(jnp.square(h - mu), axis=-1, keepdims=True)
    hn = ((h - mu) * lax.rsqrt(var + EPS)).reshape(B, T, M_WIDTH) * norm_w
    out = jax.nn.sigmoid(om) * hn + skip * xc
    return out * jax.nn.silu(zm)


def nsa_group(qn, kc, vc, ks, vs, kw, vw, gl, zn, pos_k, pos_v, w1_k, w2_k, w1_v, w2_v):
    B, T, _ = qn.shape
    G, HPG, dh = N_KV, N_HPG, N_HD
    n_cmp = (T - CMP_LEN) // CMP_STRIDE + 1
    n_slc = T // SLC_LEN
    n_top = min(SLC_TOPN, n_slc)
    n_qb = T // QBLK
    q = qn.reshape(B, T, G, HPG, dh) * (dh ** -0.5)
    kc = kc.reshape(B, T, G, dh); vc = vc.reshape(B, T, G, dh)
    ks = ks.reshape(B, T, G, dh); vs = vs.reshape(B, T, G, dh)
    kw = kw.reshape(B, T, G, dh); vw = vw.reshape(B, T, G, dh)

    cidx = np.arange(n_cmp)[:, None] * CMP_STRIDE + np.arange(CMP_LEN)[None, :]

    def compress(a, pos, w1, w2):
        blk = a[:, cidx] + pos[None, None, :, None, :]
        blk = blk.transpose(0, 1, 3, 2, 4).reshape(B, n_cmp, G, CMP_LEN * dh)
        return jax.nn.gelu(blk @ w1) @ w2

    k_cmp = compress(kc, pos_k, w1_k, w2_k)
    v_cmp = compress(vc, pos_v, w1_v, w2_v)
    cmp_end = jnp.asarray(cidx[:, -1], jnp.int32)
    cmp_mid = jnp.asarray(cidx.mean(-1), jnp.float32)
    s_start = np.arange(n_slc) * SLC_LEN
    overlap = jnp.asarray((cidx[:, 0][:, None] <= s_start[None, :] + SLC_LEN - 1)
                          & (cidx[:, -1][:, None] >= s_start[None, :]), jnp.float32)
    slopes = jnp.asarray(alibi_slopes()).reshape(G, HPG)

    ks_blk = ks.reshape(B, n_slc, SLC_LEN, G, dh).transpose(0, 3, 1, 2, 4)
    vs_blk = vs.reshape(B, n_slc, SLC_LEN, G, dh).transpose(0, 3, 1, 2, 4)
    kw_pad = jnp.pad(kw, ((0, 0), (WIN, 0), (0, 0), (0, 0)))
    vw_pad = jnp.pad(vw, ((0, 0), (WIN, 0), (0, 0), (0, 0)))
    gates = jax.nn.sigmoid(gl).reshape(B, T, G, HPG, 3)
    q_blocks = q.reshape(B, n_qb, QBLK, G, HPG, dh).swapaxes(0, 1)
    g_blocks = gates.reshape(B, n_qb, QBLK, G, HPG, 3).swapaxes(0, 1)
    starts = jnp.arange(n_qb, dtype=jnp.int32) * QBLK
    bi = jnp.arange(B)[:, None, None, None]
    gi = jnp.arange(G)[None, :, None, None]
    sidx = jnp.arange(n_slc, dtype=jnp.int32)

    def block(args):
        qb, gb, start = args
        t = start + jnp.arange(QBLK, dtype=jnp.int32)
        tf = t.astype(jnp.float32)
        s_c = jnp.einsum('bqghd,bjgd->bqghj', qb, k_cmp)
        s_c = s_c - slopes[None, None, :, :, None] * (tf[:, None] - cmp_mid[None, :])[None, :, None, None, :]
        mask_c = (cmp_end[None, :] <= t[:, None])[None, :, None, None, :]
        p_c = masked_softmax(s_c, mask_c, -1)
        o_c = jnp.einsum('bqghj,bjgd->bqghd', p_c, v_cmp)
        imp = jnp.einsum('bqghj,js->bgqs', p_c, overlap)
        cur = t // SLC_LEN
        imp = jnp.where(sidx[None, :] <= cur[:, None], imp, -jnp.inf)
        imp = jnp.where((sidx[None, :] == 0) | (sidx[None, :] == cur[:, None]), jnp.inf, imp)
        top_val, top_idx = lax.top_k(imp, n_top)
        sel_ok = top_val > -jnp.inf
        k_sel = ks_blk[bi, gi, top_idx]
        v_sel = vs_blk[bi, gi, top_idx]
        qg = qb.transpose(0, 2, 1, 3, 4)
        s_s = jnp.einsum('bgqhd,bgqnld->bgqhnl', qg, k_sel)
        pos = top_idx[..., None] * SLC_LEN + jnp.arange(SLC_LEN, dtype=jnp.int32)
        dist_s = (t[None, None, :, None, None] - pos).astype(jnp.float32)
        s_s = s_s - slopes[None, :, None, :, None, None] * dist_s[:, :, :, None]
        mask_s = ((pos <= t[None, None, :, None, None]) & sel_ok[..., None])[:, :, :, None]
        p_s = masked_softmax(s_s, mask_s, (-2, -1))
        o_s = jnp.einsum('bgqhnl,bgqnld->bgqhd', p_s, v_sel).transpose(0, 2, 1, 3, 4)
        k_win = lax.dynamic_slice_in_dim(kw_pad, start, QBLK + WIN, axis=1)
        v_win = lax.dynamic_slice_in_dim(vw_pad, start, QBLK + WIN, axis=1)
        pos_w = start - WIN + jnp.arange(QBLK + WIN, dtype=jnp.int32)
        dw = t[:, None] - pos_w[None, :]
        mask_w = ((dw >= 0) & (dw < WIN) & (pos_w[None, :] >= 0))[None, :, None, None, :]
        s_w = jnp.einsum('bqghd,bkgd->bqghk', qb, k_win)
        s_w = s_w - slopes[None, None, :, :, None] * dw.astype(jnp.float32)[None, :, None, None, :]
        p_w = masked_softmax(s_w, mask_w, -1)
        o_w = jnp.einsum('bqghk,bkgd->bqghd', p_w, v_win)
        return gb[..., 0:1] * o_c + gb[..., 1:2] * o_s + gb[..., 2:3] * o_w

    o = lax.map(block, (q_blocks, g_blocks, starts))
    o = o.swapaxes(0, 1).reshape(B, T, N_WIDTH)
    return o * jax.nn.silu(zn)


def setup_inputs(seed: int = 0) -> dict:
    key = jax.random.key(seed)
    k = jax.random.split(key, 24)
    f32 = jnp.float32
    nrm = lambda kk, shape, scale: jax.random.normal(kk, shape, f32) * scale
    L = DEPTH
    return {
        'x': nrm(k[0], (BATCH, SEQ, D_MODEL), 1.0),
        'c': nrm(k[1], (BATCH, D_MODEL), 1.0),
        'ln_g': 1.0 + nrm(k[2], (L, D_MODEL), 0.02),
        'w_ada': nrm(k[3], (L, D_MODEL, 3 * D_MODEL), 0.5 * D_MODEL ** -0.5),
        'b_ada': nrm(k[4], (L, 3 * D_MODEL), 0.02),
        'w_in': nrm(k[5], (L, D_MODEL, D_IN), D_MODEL ** -0.5),
        'b_in': nrm(k[6], (L, D_IN), 0.02),
        'm_conv_w': nrm(k[7], (L, CONV_K, M_WIDTH), CONV_K ** -0.5),
        'm_conv_b': nrm(k[8], (L, M_WIDTH), 0.02),
        'm_wq': nrm(k[9], (L, M_HEADS, M_HD, M_HD), M_HD ** -0.5),
        'm_wk': nrm(k[10], (L, M_HEADS, M_HD, M_HD), M_HD ** -0.5),
        'm_norm_w': 1.0 + nrm(k[11], (L, M_WIDTH), 0.02),
        'm_skip': 1.0 + nrm(k[12], (L, M_WIDTH), 0.02),
        'm_f_bias': jnp.linspace(3.0, 6.0, M_HEADS, dtype=f32)[None, :] + nrm(k[13], (L, M_HEADS), 0.02),
        'n_pos_k': nrm(k[14], (L, CMP_LEN, N_HD), 0.5),
        'n_pos_v': nrm(k[15], (L, CMP_LEN, N_HD), 0.5),
        'n_w1_k': nrm(k[16], (L, CMP_LEN * N_HD, CMP_HIDDEN), (CMP_LEN * N_HD) ** -0.5),
        'n_w2_k': nrm(k[17], (L, CMP_HIDDEN, N_HD), CMP_HIDDEN ** -0.5),
        'n_w1_v': nrm(k[18], (L, CMP_LEN * N_HD, CMP_HIDDEN), (CMP_LEN * N_HD) ** -0.5),
        'n_w2_v': nrm(k[19], (L, CMP_HIDDEN, N_HD), CMP_HIDDEN ** -0.5),
        'w_out': nrm(k[20], (L, D_MIX, D_MODEL), D_MIX ** -0.5),
        'final_g': 1.0 + nrm(k[21], (D_MODEL,), 0.02),
    }


def reference(x, c, ln_g, w_ada, b_ada, w_in, b_in, m_conv_w, m_conv_b, m_wq, m_wk, m_norm_w,
              m_skip, m_f_bias, n_pos_k, n_pos_v, n_w1_k, n_w2_k, n_w1_v, n_w2_v, w_out, final_g):
    out_dtype = x.dtype
    h_res = x.astype(jnp.float32)
    cf = c.astype(jnp.float32)
    offsets = [int(o) for o in np.cumsum(IN_SPLITS)[:-1]]
    for l in range(DEPTH):
        mod = jax.nn.silu(cf) @ w_ada[l] + b_ada[l]
        shift, scale, gate = jnp.split(mod, 3, axis=-1)
        h = rmsnorm(h_res, ln_g[l]) * (1.0 + scale[:, None]) + shift[:, None]
        p = h @ w_in[l] + b_in[l]
        (m_x, m_v, m_o, m_z, m_i, m_f, n_q, n_kc, n_vc, n_ks, n_vs, n_kw, n_vw, n_g, n_z) = \
            jnp.split(p, offsets, axis=-1)
        y_m = mlstm_group(m_x, m_v, m_o, m_z, m_i, m_f, m_conv_w[l], m_conv_b[l], m_wq[l], m_wk[l],
                          m_norm_w[l], m_skip[l], m_f_bias[l])
        y_n = nsa_group(n_q, n_kc, n_vc, n_ks, n_vs, n_kw, n_vw, n_g, n_z, n_pos_k[l], n_pos_v[l],
                        n_w1_k[l], n_w2_k[l], n_w1_v[l], n_w2_v[l])
        y = jnp.concatenate([y_m, y_n], axis=-1) @ w_out[l]
        h_res = h_res + gate[:, None] * y
    return rmsnorm(h_res, final_g).astype(out_dtype)
```

```python
import math
from contextlib import ExitStack

import numpy as np
import ml_dtypes

import concourse.bass as bass
import concourse.mybir as mybir
from concourse.bass_utils import run_bass_kernel_spmd

F32 = mybir.dt.float32
BF16 = mybir.dt.bfloat16
AF = mybir.ActivationFunctionType
ALU = mybir.AluOpType
bf = ml_dtypes.bfloat16

T = 4096
D = 2048
NT = 32
NST = 8
EPS = 1e-6
BIG = 1.0e30
NEGM = 30000.0
GC = 0.7978845608028654

ENG = ("pe", "act", "dve", "pool", "sp")


class Sched:
    def __init__(self, nc):
        self.nc = nc
        self.items = {e: [] for e in ENG}
        self.count = {e: 0 for e in ENG}
        self.known = {e: {} for e in ENG}
        self.clock = {}
        self.last_w = {}
        self.readers = {}
        self.bank_last = {}
        self.dma_count = {}
        self.ncc = 0
        self.sems = {}

    def _need(self, eng, ev, waits):
        if ev is None:
            return
        k, v = ev
        if k == "pe" and eng == "pe":
            return
        if self.known[eng].get(k, 0) >= v:
            return
        waits[k] = max(waits.get(k, 0), v)
        for kk, vv in self.clock.get(ev, {}).items():
            if self.known[eng].get(kk, 0) < vv:
                self.known[eng][kk] = vv
        self.known[eng][k] = max(self.known[eng].get(k, 0), v)

    def _deps(self, eng, reads, writes, banks):
        waits = {}
        for b in reads:
            self._need(eng, self.last_w.get(b), waits)
        for b in writes:
            self._need(eng, self.last_w.get(b), waits)
            for ev in self.readers.get(b, ()):
                self._need(eng, ev, waits)
        for b in banks:
            for e2, ev in self.bank_last.get(b, {}).items():
                if e2 != eng:
                    self._need(eng, ev, waits)
        return waits

    def _commit(self, eng, ev, reads, writes, banks):
        clk = dict(self.known[eng])
        clk[ev[0]] = max(clk.get(ev[0], 0), ev[1])
        self.clock[ev] = clk
        for b in reads:
            self.readers.setdefault(b, []).append(ev)
        for b in writes:
            self.last_w[b] = ev
            self.readers[b] = []
        for b in banks:
            self.bank_last.setdefault(b, {})[eng] = ev

    def op(self, eng, fn, reads=(), writes=(), banks=()):
        waits = self._deps(eng, reads, writes, banks)
        self.count[eng] += 1
        ev = (eng, self.count[eng])
        self.items[eng].append((waits, fn, (eng, 1)))
        self._commit(eng, ev, reads, writes, banks)
        return ev

    def dma(self, q, slot, fn, reads=(), writes=()):
        waits = self._deps(q, reads, writes, ())
        prev = self.dma_count.get(slot, 0)
        key = ("dma", slot)
        if prev:
            self._need(q, (key, prev), waits)
        self.dma_count[slot] = prev + 16
        ev = (key, prev + 16)
        self.items[q].append((waits, fn, (key, 16)))
        self._commit(q, ev, reads, writes, ())
        return ev

    def collective(self, fn, reads=(), writes=()):
        waits = self._deps("pool", reads, writes, ())
        self.ncc += 1
        key = ("cc", self.ncc)
        ev = (key, 1)
        self.items["pool"].append((waits, fn, (key, None)))
        self._commit("pool", ev, reads, writes, ())
        w2 = {}
        self._need("pool", ev, w2)
        self.items["pool"].append((w2, None, None))
        return ev

    def barrier(self):
        evs = [(e, self.count[e]) for e in ENG if self.count[e]]
        evs += [(("dma", s), c) for s, c in self.dma_count.items()]
        evs += [(("cc", i + 1), 1) for i in range(self.ncc)]
        for e in ENG:
            waits = {}
            for ev in evs:
                if ev[0] == e:
                    continue
                self._need(e, ev, waits)
            if waits:
                self.items[e].append((waits, None, None))

    def wait_all(self, eng, bufs):
        waits = {}
        for b in bufs:
            self._need(eng, self.last_w.get(b), waits)
        self.items[eng].append((waits, None, None))

    def emit(self, stack):
        nc = self.nc
        keys = set()
        for e in ENG:
            for waits, fn, inc in self.items[e]:
                keys.update(waits.keys())
                if inc:
                    keys.add(inc[0])
        for i, k in enumerate(sorted(keys, key=str)):
            self.sems[k] = stack.enter_context(nc.semaphore("s%d" % i))
        block = stack.enter_context(nc.Block())
        starters = {"pe": block.tensor, "act": block.scalar, "dve": block.vector,
                    "pool": block.gpsimd, "sp": block.sync}

        def make(e):
            def body(engobj):
                for waits, fn, inc in self.items[e]:
                    for k, v in waits.items():
                        engobj.wait_ge(self.sems[k], v)
                    if fn is not None:
                        ins = fn(engobj)
                        if inc[1] is None:
                            ins.then_inc(self.sems[inc[0]])
                        else:
                            ins.then_inc(self.sems[inc[0]], inc[1])
            return body

        for e in ENG:
            if self.items[e]:
                starters[e](make(e))
        return len(keys)


INPUT_SPECS = [
    ("x", [T, D], F32), ("xcol", [T, 512], F32), ("c_col", [128, 16], F32),
    ("wada", [D, 1536], F32), ("bada", [1, 1536], F32), ("lng", [1, D], F32),
    ("wm", [D, 1026], F32), ("bm_fm", [128, 2], F32), ("bm_tm", [1, 770], F32),
    ("wn", [D, 1036], F32), ("bn_fm", [128, 8], F32), ("bn_tm", [1, 396], F32),
    ("convw", [128, 8], F32), ("convb", [128, 2], F32),
    ("wq", [256, 256], F32), ("wk", [256, 256], F32),
    ("normw", [1, 256], F32), ("skip", [1, 256], F32), ("fbias", [1, 1], F32),
    ("posk2", [128, 16], F32), ("posv2", [128, 16], F32),
    ("w1k", [D, 128], F32), ("w1v", [D, 128], F32), ("w2k", [128, 64], F32), ("w2v", [128, 64], F32),
    ("wout", [D, 512], F32), ("fg", [1, 512], F32),
    ("identb", [128, 128], BF16), ("identf", [128, 128], F32), ("triu", [128, 128], F32),
    ("onesf", [128, 128], F32), ("expand", [64, T], BF16), ("e0", [64, T], BF16),
    ("rowt", [64, 512], F32), ("abt", [128, 128], F32), ("cbt", [128, 256], F32),
    ("ovl1", [128, 128], BF16), ("dgt", [128, 512], F32), ("fet", [128, 512], F32),
]


def build(debug=False, upto=3, mtiles=NST, ntiles=NST):
    nc = bass.Bass("TRN2", target_bir_lowering=False)
    I = {}
    for name, shape, dt in INPUT_SPECS:
        I[name] = nc.dram_tensor(name, shape, dt, kind="ExternalInput").ap()
    out_d = nc.dram_tensor("out", [T, 512], F32, kind="ExternalOutput").ap()
    dbg = {}
    if debug:
        dbg["ym"] = nc.dram_tensor("dbg_ym", [256, T], BF16, kind="ExternalOutput").ap()
        dbg["yn"] = nc.dram_tensor("dbg_yn", [256, T], BF16, kind="ExternalOutput").ap()
        dbg["mod"] = nc.dram_tensor("dbg_mod", [4, 1024], F32, kind="ExternalOutput").ap()
    agin = nc.dram_tensor("agin", [1, 1024], F32)
    agout = nc.dram_tensor("agout", [4, 1024], F32)
    gate_d = nc.dram_tensor("gate_d", [1, 512], F32)
    ymT_d = [nc.dram_tensor("ymT_d%d" % i, [256, T // 2], BF16) for i in range(2)]
    ym_all = [nc.dram_tensor("ym_all%d" % i, [1024, T // 2], BF16) for i in range(2)]
    ynT_d = [nc.dram_tensor("ynT_d%d" % i, [256, T // 2], BF16) for i in range(2)]
    yn_all = [nc.dram_tensor("yn_all%d" % i, [1024, T // 2], BF16) for i in range(2)]
    sq_in = nc.dram_tensor("sq_in", [128, 32], F32)
    sq_all = nc.dram_tensor("sq_all", [512, 32], F32)
    GROUPS = [[0, 1, 2, 3], [4, 5, 6, 7]]

    with ExitStack() as st:
        AW = 53200
        arena = st.enter_context(nc.sbuf_tensor("arena", [128, AW], F32))
        ps = [st.enter_context(nc.psum_tensor("ps%d" % i, [128, 512], F32)) for i in range(8)]
        psb = [p[:, :].bitcast(BF16) for p in ps]
        S = Sched(nc)
        off = [0]

        def finish(bufs):
            S.wait_all("sp", bufs)
            nsem = S.emit(st)
            build.stats = dict(nsem=nsem, counts=dict(S.count), dmas=len(S.dma_count))
            return nc

        def alloc(shape, dt):
            n = int(np.prod(shape))
            words = n if dt == F32 else (n + 1) // 2
            assert off[0] + words <= AW, ("arena overflow", off[0], words)
            ap = arena[:, off[0]:off[0] + words]
            off[0] += words
            if dt != F32:
                ap = ap.bitcast(dt)
                if 2 * words != n:
                    ap = ap[:, 0:n]
            if len(shape) == 2:
                ap = ap.rearrange("p (a b) -> p a b", a=shape[0])
            elif len(shape) == 3:
                ap = ap.rearrange("p (a b c) -> p a b c", a=shape[0], b=shape[1])
            return ap

        def alloc_at(offset, shape, dt):
            save = off[0]
            off[0] = offset
            ap = alloc(shape, dt)
            end = off[0]
            off[0] = save
            return ap, end

        def mm(out, lhsT, rhs, start, stop, bank, reads=()):
            S.op("pe", lambda e: e.matmul(out=out, lhsT=lhsT, rhs=rhs, start=start, stop=stop,
                                          skip_group_check=True), reads=reads, banks=[bank])

        def tr(out, in_, ident, bank, reads=()):
            S.op("pe", lambda e: e.transpose(out=out, in_=in_, identity=ident), reads=reads, banks=[bank])

        def act(out, in_, func, bias=None, scale=None, accum=None, reads=(), writes=(), banks=()):
            kw = {}
            if bias is not None:
                kw["bias"] = bias
            if scale is not None:
                kw["scale"] = scale
            if accum is not None:
                kw["accum_out"] = accum
            S.op("act", lambda e: e.activation(out=out, in_=in_, func=func, **kw),
                 reads=reads, writes=writes, banks=banks)

        def tt(eng, out, in0, in1, op, reads=(), writes=(), banks=()):
            S.op(eng, lambda e: e.tensor_tensor(out=out, in0=in0, in1=in1, op=op), reads=reads, writes=writes, banks=banks)

        def ts(eng, out, in0, s1, op0, s2=None, op1=None, reads=(), writes=(), banks=()):
            if op1 is None:
                S.op(eng, lambda e: e.tensor_scalar(out=out, in0=in0, scalar1=s1, scalar2=None, op0=op0),
                     reads=reads, writes=writes, banks=banks)
            else:
                S.op(eng, lambda e: e.tensor_scalar(out=out, in0=in0, scalar1=s1, scalar2=s2, op0=op0, op1=op1),
                     reads=reads, writes=writes, banks=banks)

        def stt(out, in0, scalar, in1, op0, op1, reads=(), writes=(), banks=()):
            S.op("dve", lambda e: e.scalar_tensor_tensor(out=out, in0=in0, scalar=scalar, in1=in1, op0=op0, op1=op1),
                 reads=reads, writes=writes, banks=banks)

        def cp(eng, out, in_, reads=(), writes=(), banks=()):
            S.op(eng, lambda e: e.tensor_copy(out=out, in_=in_), reads=reads, writes=writes, banks=banks)

        def memset(eng, out, val, writes=()):
            S.op(eng, lambda e: e.memset(out, val), writes=writes)

        def recip(out, in_, reads=(), writes=()):
            S.op("dve", lambda e: e.reciprocal(out=out, in_=in_), reads=reads, writes=writes)

        def dma(q, slot, out, in_, reads=(), writes=()):
            S.dma(q, slot, lambda e: e.dma_start(out=out, in_=in_), reads=reads, writes=writes)

        def asel(out, in_, pattern, base, cm, reads=(), writes=()):
            S.op("pool", lambda e: e.affine_select(out=out, in_=in_, pattern=pattern, compare_op=ALU.is_ge,
                                                   fill=0.0, base=base, channel_multiplier=cm),
                 reads=reads, writes=writes)

        identb = alloc((128,), BF16)
        identf = alloc((128,), F32)
        triu = alloc((128,), F32)
        onesf = alloc((128,), F32)
        gs_bc = alloc((D,), F32)
        shift_col = alloc((16,), F32)
        shift_bf = alloc((16,), BF16)
        shiftrep = alloc((16, 128), BF16)
        gate_bc = alloc((512,), F32)
        neghalf = alloc((32,), F32)
        ss = alloc((64,), F32)
        rstd = alloc((64,), F32)
        dma("sp", "c0", identb, I["identb"], writes=["identb"])
        dma("sp", "c1", identf, I["identf"], writes=["identf"])
        dma("sp", "c2", triu, I["triu"], writes=["triu"])
        dma("sp", "c3", onesf, I["onesf"], writes=["onesf"])
        memset("dve", neghalf, -0.5, writes=["neghalf"])
        wn_bf = alloc((16, 1036), BF16)
        w1 = [alloc((16, 128), BF16) for _ in range(2)]
        w2k_bf = alloc((64,), BF16)
        w2v_bf = alloc((64,), BF16)
        pos2 = [alloc((16,), BF16) for _ in range(2)]
        G0 = off[0]
        WTOP = AW - (8 * 1026 + 2 * 256 + 16)
        wm_bf, e_ = alloc_at(WTOP, (16, 1026), BF16)
        wq_bf, e_ = alloc_at(e_, (2, 256), BF16)
        wk_bf, e_ = alloc_at(e_, (2, 256), BF16)
        assert e_ <= AW

        c_col = alloc((16,), F32)
        th16 = alloc((16,), F32)
        sc_f = alloc((16,), F32)
        bada = alloc((1536,), F32)
        mod_sb = alloc((1536,), F32)
        lng_bc = alloc((D,), F32)
        scale_bc = alloc((D,), F32)
        sh16 = alloc((128,), F32)
        wst = [alloc((4, 1536), F32) for _ in range(2)]
        dma("sp", "c0", c_col, I["c_col"], writes=["c_col"])
        wada_v = I["wada"].rearrange("(k p) n -> p k n", p=128)
        for q4 in range(2):
            dma("sp", "w%d" % q4, wst[q4][:, :, 0:1024], wada_v[:, 4 * q4:4 * q4 + 4, 0:1024], writes=[("wst", q4)])
        dma("sp", "c1", bada[0:1, :], I["bada"], writes=["bada"])
        dma("sp", "c2", lng_bc, I["lng"].partition_broadcast(128), writes=["lng_bc"])
        assert off[0] <= WTOP, ("phase-0 scratch overlaps preloaded weights", off[0], WTOP)
        wm_v = I["wm"].rearrange("(k p) n -> p k n", p=128)
        for q4 in range(4):
            dma("pool", "m%d" % q4, wm_bf[:, 4 * q4:4 * q4 + 4, :], wm_v[:, 4 * q4:4 * q4 + 4, :], writes=[("wm", q4)])
        dma("pool", "m4", wq_bf, I["wq"].rearrange("(k p) n -> p k n", p=128), writes=["wq"])
        dma("pool", "m5", wk_bf, I["wk"].rearrange("(k p) n -> p k n", p=128), writes=["wk"])
        act(th16, c_col, AF.Tanh, scale=0.5, reads=["c_col"], writes=["th16"])
        ts("dve", th16, th16, 0.5, ALU.mult, 0.5, ALU.add, reads=["th16"], writes=["th16"])
        tt("dve", sc_f, th16, c_col, ALU.mult, reads=["th16", "c_col"], writes=["sc_f"])
        for q4 in range(4):
            wb = wst[q4 % 2]
            for kk in range(4):
                k = 4 * q4 + kk
                for nchk in range(2):
                    mm(ps[nchk][0:1, :], sc_f[:, k:k + 1], wb[:, kk, nchk * 512:(nchk + 1) * 512],
                       k == 0, k == 15, nchk, reads=["sc_f", ("wst", q4 % 2)])
            if q4 + 2 < 4:
                dma("sp", "w%d" % (q4 % 2), wst[q4 % 2][:, :, 0:1024], wada_v[:, 4 * (q4 + 2):4 * (q4 + 2) + 4, 0:1024],
                    writes=[("wst", q4 % 2)])
        for nchk in range(2):
            tt("dve", mod_sb[0:1, nchk * 512:(nchk + 1) * 512], ps[nchk][0:1, :], bada[0:1, nchk * 512:(nchk + 1) * 512],
               ALU.add, reads=["bada"], writes=[("mod", nchk)], banks=[nchk])
        dma("sp", "c3", agin.ap(), mod_sb[0:1, 0:1024], reads=[("mod", 0), ("mod", 1)], writes=["agin"])
        S.collective(lambda e: e.collective_compute("AllGather", ALU.bypass, replica_groups=GROUPS,
                                                    ins=[agin.ap().opt()], outs=[agout.ap().opt()]),
                     reads=["agin"], writes=["agout"])
        for q4 in range(4):
            dma("sp", "w%d" % (q4 % 2), wst[q4 % 2][:, :, 0:512], wada_v[:, 4 * q4:4 * q4 + 4, 1024:1536], writes=[("wst", q4 % 2)])
            for kk in range(4):
                k = 4 * q4 + kk
                mm(ps[2][0:1, :], sc_f[:, k:k + 1], wst[q4 % 2][:, kk, 0:512], k == 0, k == 15, 2, reads=["sc_f", ("wst", q4 % 2)])
        tt("dve", mod_sb[0:1, 1024:1536], ps[2][0:1, :], bada[0:1, 1024:1536], ALU.add, reads=["bada"], writes=[("mod", 2)], banks=[2])
        agflat = agout.ap().rearrange("a (o n) -> o (a n)", o=1)
        dma("sp", "c0", scale_bc, agflat[:, 2048:4096].partition_broadcast(128), reads=["agout"], writes=["scale_bc"])
        dma("sp", "c1", sh16[0:16, :], agout.ap().rearrange("a (k n) -> (a k) n", n=128)[0:16, :],
            reads=["agout"], writes=["sh16"])
        dma("sp", "c4", gate_d.ap(), mod_sb[0:1, 1024:1536], reads=[("mod", 2)], writes=["gate_d"])
        dma("sp", "c4", gate_bc, gate_d.ap().partition_broadcast(128), reads=["gate_d"], writes=["gate_bc"])
        if debug:
            dma("sp", "c5", dbg["mod"], agout.ap(), reads=["agout"], writes=["dbg_mod"])
        stt(gs_bc, scale_bc, 1.0, lng_bc, ALU.add, ALU.mult, reads=["scale_bc", "lng_bc"], writes=["gs_bc"])
        tr(ps[3][:, 0:16], sh16[0:16, :], identf[0:16, 0:16], 3, reads=["sh16", "identf"])
        cp("dve", shift_col, ps[3][:, 0:16], writes=["shift_col"], banks=[3])
        cp("dve", shift_bf, shift_col, reads=["shift_col"], writes=["shift_bf"])
        for k in range(16):
            ts("dve", shiftrep[:, k, :], onesf, shift_col[:, k:k + 1], ALU.mult,
               reads=["onesf", "shift_col"], writes=["shiftrep"])
        if upto == 0:
            return finish(["dbg_mod"])
        S.barrier()
        off[0] = G0

        def front_end(n, a, xs, xg, junk, hT, hkey, sscol, gen=False):
            g = _front_end(n, a, xs, xg, junk, hT, hkey, sscol)
            if gen:
                return g
            for _ in g:
                pass

        def _front_end(n, a, xs, xg, junk, hT, hkey, sscol):
            xb = xs[n % 2]
            gb = xg[n % 2]
            dma("sp", "x%d" % (n % 2), xb, I["x"][n * 128:(n + 1) * 128, :], writes=[("xs", n % 2)])
            act(junk, xb, AF.Square, accum=ss[:, sscol:sscol + 1], reads=[("xs", n % 2)], writes=[("ss", sscol)])
            ts("dve", rstd[:, sscol:sscol + 1], ss[:, sscol:sscol + 1], 1.0 / D, ALU.mult, EPS, ALU.add,
               reads=[("ss", sscol)], writes=[("ms", sscol)])
            tt("pool", rstd[:, sscol:sscol + 1], rstd[:, sscol:sscol + 1], neghalf[:, 0:1], ALU.pow,
               reads=[("ms", sscol), "neghalf"], writes=[("rstd", sscol)])
            yield
            stt(gb, xb, rstd[:, sscol:sscol + 1], gs_bc, ALU.mult, ALU.mult,
                reads=[("xs", n % 2), ("rstd", sscol), "gs_bc"], writes=[("xg", n % 2)])
            yield
            yield
            for r in range(2):
                for kk in range(8):
                    k = 8 * r + kk
                    tr(psb[0][:, kk * 128:(kk + 1) * 128], gb[:, k * 128:(k + 1) * 128], identb, 0,
                       reads=[("xg", n % 2), "identb"])
                src = psb[0][:, :].rearrange("p (k t) -> p k t", k=8)
                dst = hT[:, 8 * r:8 * r + 8, a * 128:(a + 1) * 128]
                if r == 0:
                    act(dst, src, AF.Copy, writes=[(hkey, a, r)], banks=[0])
                else:
                    cp("dve", dst, src, writes=[(hkey, a, r)], banks=[0])
                yield

        PM0 = off[0]
        xs = [alloc((D,), F32) for _ in range(2)]
        xg = [alloc((D,), BF16) for _ in range(2)]
        junk = alloc((D,), BF16)
        hT = alloc((16, 512), BF16)
        bm_fm = alloc((2,), F32)
        bmtm_bc = alloc((770,), F32)
        biasTM = alloc((770,), F32)
        biasFM = alloc((2,), F32)
        convw = alloc((2, 4), F32)
        convb = alloc((2,), F32)
        normw_bc = alloc((256,), F32)
        skip_bc = alloc((256,), F32)
        fbias = alloc((1,), F32)
        nfb = alloc((1,), F32)
        mx = alloc((2, 515), F32)
        cacc = alloc((2, 512), F32)
        cth = alloc((2, 512), F32)
        xcT = alloc((2, 512), BF16)
        qT = alloc((2, 512), BF16)
        kT = alloc((2, 512), BF16)
        Cst = alloc((2, 257), F32)
        Cbf = alloc((2, 257), BF16)
        if_sb = alloc((4, 2), F32)
        g_sp = alloc((4,), F32)
        g_t = alloc((4,), F32)
        g_u = alloc((4,), F32)
        g_w = alloc((4,), F32)
        g_dec = alloc((4,), F32)
        PT = []
        for _ in range(4):
            PT.append(dict(
                vaug=alloc((257,), BF16), to=alloc((256,), F32), tho=alloc((256,), F32),
                tz=alloc((256,), F32), thz=alloc((256,), F32), ktm=alloc((256,), BF16),
                xstm=alloc((256,), F32), Ssb=alloc((128,), BF16), hv=alloc((256,), F32),
                sm=alloc((16,), F32), ym=alloc((256,), BF16), ymT=alloc((2, 128), BF16),
                z1=alloc((256,), F32)))
        assert off[0] <= WTOP, ("phase M overlaps preloaded weights", off[0], WTOP)
        assert off[0] <= AW

        dma("sp", "c0", bm_fm, I["bm_fm"], writes=["bm_fm"])
        dma("sp", "c1", bmtm_bc, I["bm_tm"].partition_broadcast(128), writes=["bmtm_bc"])
        dma("sp", "c2", convw, I["convw"].rearrange("p (c j) -> p c j", c=2), writes=["convw"])
        dma("sp", "c3", convb, I["convb"], writes=["convb"])
        dma("sp", "c4", normw_bc, I["normw"].partition_broadcast(128), writes=["normw_bc"])
        dma("sp", "c0", skip_bc, I["skip"].partition_broadcast(128), writes=["skip_bc"])
        dma("sp", "c1", fbias, I["fbias"].partition_broadcast(128), writes=["fbias"])
        ts("dve", nfb, fbias, -1.0, ALU.mult, reads=["fbias"], writes=["nfb"])
        WM_ALL = [("wm", q) for q in range(4)]
        for g, (c0, w) in enumerate(((256, 512), (768, 258))):
            for k in range(16):
                mm(ps[1 + g][:, 0:w], shiftrep[:, k, :], wm_bf[:, k, c0:c0 + w], k == 0, k == 15, 1 + g,
                   reads=["shiftrep", ("wm", k // 4)])
            tt("dve", biasTM[:, c0 - 256:c0 - 256 + w], ps[1 + g][:, 0:w], bmtm_bc[:, c0 - 256:c0 - 256 + w], ALU.add,
               reads=["bmtm_bc"], writes=["biasTM"], banks=[1 + g])
        for c in range(2):
            for k in range(16):
                mm(ps[3][:, c:c + 1], wm_bf[:, k, c * 128:(c + 1) * 128], shift_bf[:, k:k + 1], k == 0, k == 15, 3,
                   reads=["shift_bf", ("wm", k // 4)])
        tt("dve", biasFM, ps[3][:, 0:2], bm_fm, ALU.add, reads=["bm_fm"], writes=["biasFM"], banks=[3])
        memset("dve", mx[:, :, 0:3], 0.0, writes=["mx_carry"])
        memset("dve", Cst, 0.0, writes=[("Cst", 0), ("Cst", 1)])
        memset("pool", Cbf, 0.0, writes=["Cbf"])
        for pb in PT:
            memset("pool", pb["vaug"][:, 256:257], 1.0, writes=[])

        ymT_v = [t_.ap().rearrange("(c p) t -> p c t", p=128) for t_ in ymT_d]
        XC = [("xcT", 0), ("xcT", 1)]
        HT_ALL = [("hT", a, r) for a in range(4) for r in range(2)]

        def run_interleaved(*gens):
            gens = [g for g in gens if g is not None]
            while gens:
                for g in list(gens):
                    try:
                        next(g)
                    except StopIteration:
                        gens.remove(g)

        def FE_gen(s_):
            for a in range(4):
                for _ in front_end(4 * s_ + a, a, xs, xg, junk, hT, "hT", 4 * s_ + a, gen=True):
                    yield

        def FM_m(s_):
            for c in range(2):
                bk = 1 + c
                for k in range(16):
                    mm(ps[bk][:, :], wm_bf[:, k, c * 128:(c + 1) * 128], hT[:, k, :], k == 0, k == 15, bk,
                       reads=HT_ALL + [("wm", k // 4)])
                act(mx[:, c, 3:515], ps[bk][:, :], AF.Identity, bias=biasFM[:, c:c + 1],
                    reads=["biasFM", "mx_carry"], writes=[("mx", c)], banks=[bk])
            for c in range(2):
                ts("dve", cacc[:, c, :], mx[:, c, 0:512], convw[:, c, 0:1], ALU.mult, convb[:, c:c + 1], ALU.add,
                   reads=[("mx", c), "mx_carry", "convw", "convb"], writes=[("cacc", c)])
            for j in range(1, 4):
                for c in range(2):
                    stt(cacc[:, c, :], mx[:, c, j:j + 512], convw[:, c, j:j + 1], cacc[:, c, :], ALU.mult, ALU.add,
                        reads=[("mx", c), "mx_carry", ("cacc", c)], writes=[("cacc", c)])
            for c in range(2):
                act(cth[:, c, :], cacc[:, c, :], AF.Tanh, scale=0.5, reads=[("cacc", c)], writes=[("cth", c)])
            for c in range(2):
                ts("dve", cth[:, c, :], cth[:, c, :], 0.5, ALU.mult, 0.5, ALU.add, reads=[("cth", c)], writes=[("cth", c)])
            for c in range(2):
                tt("dve", xcT[:, c, :], cth[:, c, :], cacc[:, c, :], ALU.mult,
                   reads=[("cth", c), ("cacc", c)], writes=[("xcT", c)])
            cp("dve", mx[:, :, 0:3], mx[:, :, 512:515], reads=[("mx", 0), ("mx", 1)], writes=["mx_carry"])
            for ec in range(2):
                for (wt, dst, key, bk) in ((wq_bf, qT, "qT", 1), (wk_bf, kT, "kT", 2)):
                    for dk in range(2):
                        mm(ps[bk][:, :], wt[:, dk, ec * 128:(ec + 1) * 128], xcT[:, dk, :], dk == 0, dk == 1, bk,
                           reads=XC + ["wq", "wk"])
                    if key == "qT":
                        act(dst[:, ec, :], ps[bk][:, :], AF.Copy, writes=[(key, ec)], banks=[bk])
                    else:
                        ts("dve", dst[:, ec, :], ps[bk][:, :], 1.0 / 16.0, ALU.mult, writes=[(key, ec)], banks=[bk])

        def gates_st(s_):
            for a in range(4):
                for k in range(16):
                    mm(ps[3][:, 8 + 2 * a:10 + 2 * a], hT[:, k, a * 128:(a + 1) * 128], wm_bf[:, k, 1024:1026], k == 0, k == 15, 3,
                       reads=[("hT", a, 0), ("hT", a, 1), ("wm", 3)])
            for a in range(4):
                tt("dve", if_sb[:, a, :], ps[3][:, 8 + 2 * a:10 + 2 * a], biasTM[:, 768:770], ALU.add,
                   reads=["biasTM"], writes=["if"], banks=[3])
            act(g_sp, if_sb[:, :, 1], AF.Exp, bias=nfb, scale=-1.0, reads=["if", "nfb"], writes=["gsp"])
            act(g_sp, g_sp, AF.Ln, bias=1.0, reads=["gsp"], writes=["gsp"])
            mm(ps[3][:, 0:4], triu, g_sp, True, True, 3, reads=["triu", "gsp"])
            mm(ps[3][:, 4:8], onesf, g_sp, True, True, 3, reads=["onesf", "gsp"])
            tt("dve", g_t, ps[3][:, 0:4], if_sb[:, :, 0], ALU.add, reads=["if"], writes=["gt"], banks=[3])
            act(g_w, ps[3][:, 0:4], AF.Exp, scale=-1.0, writes=["gw"], banks=[3])
            act(g_dec, ps[3][:, 4:8], AF.Exp, scale=-1.0, writes=["gdec"], banks=[3])
            act(g_u, g_t, AF.Exp, reads=["gt"], writes=["gu"])

        def A1_gen(s_, a):
            n = 4 * s_ + a
            pb = PT[n % 4]
            pk = ("pt", n % 4)

            def evac0():
                tt("dve", pb["vaug"][:, 0:256], ps[1][:, 0:256], biasTM[:, 0:256], ALU.add,
                   reads=["biasTM"], writes=[(pk, "vaug")], banks=[1])
                tt("dve", pb["to"], ps[1][:, 256:512], biasTM[:, 256:512], ALU.add,
                   reads=["biasTM"], writes=[(pk, "to")], banks=[1])

            for g, (c0, w) in enumerate(((256, 512), (768, 256))):
                bk = 1 + g
                for k in range(16):
                    mm(ps[bk][:, 0:w], hT[:, k, a * 128:(a + 1) * 128], wm_bf[:, k, c0:c0 + w], k == 0, k == 15, bk,
                       reads=[("hT", a, 0), ("hT", a, 1), ("wm", k // 4)])
                    if k % 4 == 3:
                        yield
                    if g == 1 and k == 7:
                        evac0()
            yield
            act(pb["tho"], pb["to"], AF.Tanh, scale=0.5, reads=[(pk, "to")], writes=[(pk, "tho")])
            yield
            tt("dve", pb["tz"], ps[2][:, 0:256], biasTM[:, 512:768], ALU.add,
               reads=["biasTM"], writes=[(pk, "tz")], banks=[2])
            yield
            act(pb["thz"], pb["tz"], AF.Tanh, scale=0.5, reads=[(pk, "tz")], writes=[(pk, "thz")])
            yield
            yield
            stt(pb["z1"], pb["thz"], 1.0, pb["tz"], ALU.add, ALU.mult,
                reads=[(pk, "thz"), (pk, "tz")], writes=[(pk, "z1")])

        def A2_gen(s_, a):
            n = 4 * s_ + a
            pb = PT[n % 4]
            pk = ("pt", n % 4)
            for dk in range(2):
                mm(ps[7][:, 0:256], xcT[:, dk, a * 128:(a + 1) * 128], wk_bf[:, dk, :], dk == 0, dk == 1, 7,
                   reads=XC + ["wk"])
            for dk in range(2):
                tr(psb[7][:, 512 + dk * 128:512 + (dk + 1) * 128], xcT[:, dk, a * 128:(a + 1) * 128], identb, 7,
                   reads=XC + ["identb"])
            for ec in range(2):
                mm(ps[3][:, 128:256], kT[:, ec, a * 128:(a + 1) * 128], qT[:, ec, a * 128:(a + 1) * 128],
                   ec == 0, ec == 1, 3, reads=[("kT", ec), ("qT", ec)])
            yield
            yield
            yield
            tt("dve", pb["xstm"], psb[7][:, 512:768], skip_bc, ALU.mult, reads=["skip_bc"], writes=[(pk, "xstm")], banks=[7])
            yield
            ts("dve", pb["ktm"], ps[7][:, 0:256], g_u[:, a:a + 1], ALU.mult, 1.0 / 16.0, ALU.mult,
               reads=["gu"], writes=[(pk, "ktm")], banks=[7])
            yield
            stt(pb["Ssb"], ps[3][:, 128:256], g_u[:, a:a + 1], triu, ALU.mult, ALU.mult,
                reads=["gu", "triu"], writes=[(pk, "Ssb")], banks=[3])
            yield

        def B_gen(s_, a):
            n = 4 * s_ + a
            pb = PT[n % 4]
            pk = ("pt", n % 4)
            sm = pb["sm"]
            mm(ps[4][:, 0:257], pb["Ssb"], pb["vaug"], True, False, 4, reads=[(pk, "Ssb"), (pk, "vaug")])
            for dc in range(2):
                mm(ps[4][:, 0:257], qT[:, dc, a * 128:(a + 1) * 128], Cbf[:, dc, :], False, dc == 1, 4,
                   reads=[("qT", dc), "Cbf"])
            for dc in range(2):
                mm(ps[5 + dc][:, 0:257], pb["ktm"][:, dc * 128:(dc + 1) * 128], pb["vaug"], True, True, 5 + dc,
                   reads=[(pk, "ktm"), (pk, "vaug")])
            yield
            for dc in range(2):
                tt("dve", Cst[:, dc, :], ps[5 + dc][:, 0:257], Cst[:, dc, :], ALU.add,
                   reads=[("Cst", dc)], writes=[("Cst", dc)], banks=[5 + dc])
            tt("dve", sm[:, 0:1], ps[4][:, 256:257], g_w[:, a:a + 1], ALU.mult, reads=["gw"], writes=[(pk, "sm0")], banks=[4])
            yield
            for dc in range(2):
                act(Cbf[:, dc, :], Cst[:, dc, :], AF.Copy, scale=g_dec[:, a:a + 1], reads=[("Cst", dc), "gdec"], writes=["Cbf"])
            ts("dve", sm[:, 14:15], sm[:, 0:1], -1.0, ALU.mult, reads=[(pk, "sm0")], writes=[(pk, "sm14")])
            yield
            for dc in range(2):
                ts("dve", Cst[:, dc, :], Cst[:, dc, :], g_dec[:, a:a + 1], ALU.mult, reads=[("Cst", dc), "gdec"], writes=[("Cst", dc)])
            tt("dve", sm[:, 1:2], sm[:, 0:1], sm[:, 14:15], ALU.max, reads=[(pk, "sm0"), (pk, "sm14")], writes=[(pk, "sm1")])
            yield
            ts("dve", sm[:, 1:2], sm[:, 1:2], 1.0, ALU.max, reads=[(pk, "sm1")], writes=[(pk, "sm1")])
            yield
            recip(sm[:, 2:3], sm[:, 1:2], reads=[(pk, "sm1")], writes=[(pk, "sm2")])
            yield
            tt("dve", sm[:, 3:4], sm[:, 2:3], g_w[:, a:a + 1], ALU.mult, reads=[(pk, "sm2"), "gw"], writes=[(pk, "sm3")])
            yield
            act(pb["hv"], ps[4][:, 0:256], AF.Copy, scale=sm[:, 3:4], reads=[(pk, "sm3")], writes=[(pk, "hv")], banks=[4])
            yield

        def B2_gen(s_, a):
            n = 4 * s_ + a
            pb = PT[n % 4]
            pk = ("pt", n % 4)
            sm = pb["sm"]
            S.op("dve", lambda e, pb=pb, sm=sm: e.bn_stats(out=sm[:, 4:10], in_=pb["hv"]), reads=[(pk, "hv")], writes=[(pk, "bn6")])
            yield
            S.op("dve", lambda e, sm=sm: e.bn_aggr(out=sm[:, 10:12], in_=sm[:, 4:10]), reads=[(pk, "bn6")], writes=[(pk, "mv")])
            yield
            ts("dve", sm[:, 12:13], sm[:, 11:12], EPS, ALU.add, reads=[(pk, "mv")], writes=[(pk, "ve")])
            yield
            tt("pool", sm[:, 13:14], sm[:, 12:13], neghalf[:, 0:1], ALU.pow, reads=[(pk, "ve"), "neghalf"], writes=[(pk, "rs2")])
            yield
            ts("dve", pb["hv"], pb["hv"], sm[:, 10:11], ALU.subtract, sm[:, 13:14], ALU.mult,
               reads=[(pk, "hv"), (pk, "mv"), (pk, "rs2")], writes=[(pk, "hv")])
            yield
            tt("dve", pb["hv"], pb["hv"], normw_bc, ALU.mult, reads=[(pk, "hv"), "normw_bc"], writes=[(pk, "hv")])
            yield
            stt(pb["hv"], pb["tho"], 1.0, pb["hv"], ALU.add, ALU.mult, reads=[(pk, "tho"), (pk, "hv")], writes=[(pk, "hv")])
            yield
            stt(pb["hv"], pb["hv"], 0.5, pb["xstm"], ALU.mult, ALU.add, reads=[(pk, "hv"), (pk, "xstm")], writes=[(pk, "hv")])
            yield
            stt(pb["ym"], pb["hv"], 0.5, pb["z1"], ALU.mult, ALU.mult, reads=[(pk, "hv"), (pk, "z1")], writes=[(pk, "ym")])
            yield
            for dc in range(2):
                tr(psb[7][:, dc * 128:(dc + 1) * 128], pb["ym"][:, dc * 128:(dc + 1) * 128], identb, 7,
                   reads=[(pk, "ym"), "identb"])
            yield
            cp("dve", pb["ymT"], psb[7][:, 0:256].rearrange("p (c t) -> p c t", c=2), writes=[(pk, "ymT")], banks=[7])
            dma("pool", "y%d" % (n % 2), ymT_v[n // 16][:, :, (n % 16) * 128:(n % 16 + 1) * 128], pb["ymT"], reads=[(pk, "ymT")], writes=[("ymT_d", n // 16)])

        def prefetch_nsa_weights():
            wn_v = I["wn"].rearrange("(k p) n -> p k n", p=128)
            for q4 in range(4):
                dma("pool", "v%d" % q4, wn_bf[:, 4 * q4:4 * q4 + 4, :], wn_v[:, 4 * q4:4 * q4 + 4, :], writes=[("wn", q4)])
            dma("pool", "v4", w1[0], I["w1k"].rearrange("(m p) h -> p m h", p=128), writes=[("w1", 0)])
            dma("pool", "v5", w1[1], I["w1v"].rearrange("(m p) h -> p m h", p=128), writes=[("w1", 1)])
            dma("pool", "v6", w2k_bf, I["w2k"], writes=[("w2", 0)])
            dma("pool", "v8", w2v_bf, I["w2v"], writes=[("w2", 1)])
            dma("pool", "v7", pos2[0], I["posk2"], writes=[("pos2", 0)])
            dma("pool", "v9", pos2[1], I["posv2"], writes=[("pos2", 1)])

        if mtiles:
            run_interleaved(FE_gen(0))
        if not mtiles:
            prefetch_nsa_weights()
        for s_ in range(mtiles):
            FM_m(s_)
            gates_st(s_)
            if s_ == 0:
                prefetch_nsa_weights()
            run_interleaved(A1_gen(s_, 0))
            run_interleaved(A1_gen(s_, 1), A2_gen(s_, 0))
            run_interleaved(A1_gen(s_, 2), A2_gen(s_, 1), B_gen(s_, 0))
            for a in range(4):
                g1 = A1_gen(s_, a + 3) if a + 3 < 4 else (FE_gen(s_ + 1) if (a == 1 and s_ + 1 < mtiles) else None)
                g2 = A2_gen(s_, a + 2) if a + 2 < 4 else None
                g3 = B_gen(s_, a + 1) if a + 1 < 4 else None
                run_interleaved(g1, g2, g3, B2_gen(s_, a))
        if debug:
            for hf in range(2):
                if mtiles > 4 * hf:
                    w_ = min(2048, 512 * mtiles - 2048 * hf)
                    dma("sp", "c5", dbg["ym"][:, 2048 * hf:2048 * hf + w_], ymT_d[hf].ap()[:, 0:w_], reads=[("ymT_d", hf)], writes=["dbg_ym"])
        if upto == 1:
            return finish(["dbg_ym", "dbg_mod"])
        S.barrier()
        off[0] = PM0

        xs = [alloc((D,), F32) for _ in range(2)]
        xg = [alloc((D,), BF16) for _ in range(2)]
        junk = alloc((D,), BF16)
        hTn = [alloc((16, 512), BF16) for _ in range(2)]
        bn_fm = alloc((8,), F32)
        bnfm_raw = alloc((8,), F32)
        bq8 = alloc((4,), F32)
        bntm_bc = alloc((396,), F32)
        biasTMn = alloc((396,), F32)
        biasFMn = alloc((8,), F32)
        Rb = [alloc((4, 4, 128), BF16) for _ in range(2)]
        KE = alloc((T,), BF16)
        KWE = alloc((T,), BF16)
        c2T = [alloc((T + 2,), BF16) for _ in range(2)]
        vsaug = alloc((NT, 65), BF16)
        vwaug = alloc((NT, 65), BF16)
        kcmpE = alloc((256,), BF16)
        gkv = [alloc((256,), BF16) for _ in range(2)]
        vaug = alloc((2, 128), BF16)
        hb16 = alloc((16,), F32)
        rowt = alloc((4, 128), F32)
        abt = alloc((128,), F32)
        cbt = alloc((256,), F32)
        dgt = alloc((4, 128), F32)
        fet = alloc((4, 128), F32)
        Zn = [alloc((4, 256), F32) for _ in range(2)]
        gtn = [alloc((4, 12), F32) for _ in range(2)]
        tzn = [alloc((256,), F32) for _ in range(2)]
        thzn = [alloc((256,), F32) for _ in range(2)]
        tg = [alloc((12,), F32) for _ in range(2)]
        Eb = [alloc((4, 128), BF16) for _ in range(4)]
        tmpE = [alloc((4, 128), F32) for _ in range(2)]
        gch = alloc((4, 64), F32)
        hb64 = alloc((64,), F32)
        imp = alloc((64,), F32)
        rep = alloc((64,), F32)
        m8 = alloc((16,), F32)
        selw = alloc((128,), BF16)
        rz = [alloc((3, 4), F32) for _ in range(2)]
        coef = [alloc((3, 4), F32) for _ in range(2)]
        yacc = [alloc((4, 64), F32) for _ in range(2)]
        yn = [alloc((256,), BF16) for _ in range(2)]
        ynT = [alloc((2, 128), BF16) for _ in range(2)]
        assert off[0] <= AW
        build.arena_n = off[0]

        dma("sp", "c0", bnfm_raw, I["bn_fm"], writes=["bnfm_raw"])
        dma("sp", "c1", bntm_bc, I["bn_tm"].partition_broadcast(128), writes=["bntm_bc"])
        dma("sp", "c2", KE[64:128, :], I["expand"], writes=["KEc"])
        dma("sp", "c3", KWE[64:128, :], I["e0"], writes=["KWEc"])
        dma("sp", "c4", kcmpE[64:128, :], I["e0"][:, 0:256], writes=["kcmpEc"])
        dma("sp", "c0", rowt[64:128, :, :], I["rowt"].rearrange("p (h q) -> p h q", h=4), writes=["rowt"])
        dma("sp", "c1", abt, I["abt"], writes=["abt"])
        dma("sp", "c2", cbt, I["cbt"], writes=["cbt"])
        dma("sp", "c0", dgt, I["dgt"].rearrange("p (h q) -> p h q", h=4), writes=["dgt"])
        dma("sp", "c1", fet, I["fet"].rearrange("p (h q) -> p h q", h=4), writes=["fet"])
        dma("sp", "c3", vaug[:, 0, 64:128], I["ovl1"][:, 0:64], writes=["vaugc"])
        dma("sp", "c4", vaug[:, 1, 64:128], I["ovl1"][:, 64:128], writes=["vaugc"])
        WN_ALL = [("wn", q) for q in range(4)]
        for k in range(16):
            mm(ps[1][:, 0:396], shiftrep[:, k, :], wn_bf[:, k, 640:1036], k == 0, k == 15, 1, reads=["shiftrep", ("wn", k // 4)])
        tt("dve", biasTMn, ps[1][:, 0:396], bntm_bc, ALU.add, reads=["bntm_bc"], writes=["biasTMn"], banks=[1])
        FMC = [(0, 64), (64, 64), (128, 64), (192, 64), (256, 128), (384, 128), (512, 64), (576, 64)]
        memset("dve", biasFMn, 0.0, writes=["biasFMn"])
        for ci, (c0, M) in enumerate(FMC):
            for k in range(16):
                mm(ps[3][0:M, ci:ci + 1], wn_bf[:, k, c0:c0 + M], shift_bf[:, k:k + 1], k == 0, k == 15, 3,
                   reads=["shift_bf", ("wn", k // 4)])
            tt("dve", biasFMn[0:M, ci:ci + 1], ps[3][0:M, ci:ci + 1], bnfm_raw[0:M, ci:ci + 1], ALU.add,
               reads=["bnfm_raw", "biasFMn"], writes=["biasFMn"], banks=[3])
        ts("dve", bq8, biasFMn[:, 0:4], 0.125, ALU.mult, reads=["biasFMn"], writes=["bq8"])
        memset("dve", hb64, 0.0, writes=["hb64"])
        for kv in range(2):
            for m in range(16):
                mm(ps[2][:, kv:kv + 1], w1[kv][:, m, :], pos2[kv][:, m:m + 1], m == 0, m == 15, 2, reads=[("w1", 0), ("w1", 1), ("pos2", 0), ("pos2", 1)])
            ts("dve", hb64[:, kv * 32:(kv + 1) * 32], hb64[:, kv * 32:(kv + 1) * 32], ps[2][:, kv:kv + 1], ALU.add,
               reads=["hb64"], writes=["hb64"], banks=[2])
        for b2 in range(2):
            for a in range(4):
                cp("dve", Rb[b2][64:128, a, :, :], rowt[64:128, :, :], reads=["rowt"], writes=[("Rm", b2, a)])
        memset("dve", selw[:, 0:64], 0.0, writes=["selw0"])
        memset("dve", vsaug[:, :, 64:65], 1.0, writes=["vs1"])
        memset("dve", vwaug[:, :, 64:65], 1.0, writes=["vw1"])
        memset("pool", gkv[0], 0.0, writes=["gk"])
        memset("pool", gkv[1], 0.0, writes=["gv"])
        memset("pool", c2T[0][:, 0:2], 0.0, writes=[])
        memset("pool", c2T[1][:, 0:2], 0.0, writes=[])
        memset("pool", imp[:, 63:64], 0.0, writes=[])

        ynT_v = [t_.ap().rearrange("(c p) t -> p c t", p=128) for t_ in ynT_d]
        scc = [0]
        ebc = [0]

        tmc = [0]
        bgc = [0]
        SCB = (1, 2, 3, 4)
        INPROJ_STEPS = 60
        pending_tail = []
        cbt_v = cbt.rearrange("p (h j n) -> p h j n", h=4, j=2)
        abt_v = abt.rearrange("p (h r) -> p h r", h=4)

        def compress_st(s_):
            j0 = max(0, 32 * s_ - 1)
            j1 = 32 * s_ + 30
            nb = j1 - j0 + 1
            c2keys = [("c2T", kv, s2) for kv in range(2) for s2 in (s_ - 1, s_) if s2 >= 0]
            for kv in range(2):
                for m in range(16):
                    col = 16 * j0 + 2 * m + 1
                    mm(ps[0][:, kv * 32:kv * 32 + nb], w1[kv][:, m, :], c2T[kv][:, col:col + 16 * (nb - 1) + 1:16],
                       m == 0, m == 15, 0, reads=[("w1", 0), ("w1", 1)] + c2keys)
            g0_, g1_, g2_, g3_ = gch[:, 0, :], gch[:, 1, :], gch[:, 2, :], gch[:, 3, :]
            if nb < 32:
                memset("dve", g0_, 0.0, writes=["g0"])
                for kv in range(2):
                    tt("dve", g0_[:, kv * 32:kv * 32 + nb], ps[0][:, kv * 32:kv * 32 + nb], hb64[:, kv * 32:kv * 32 + nb], ALU.add,
                       reads=["hb64", "g0"], writes=["g0"], banks=[0])
            else:
                tt("dve", g0_, ps[0][:, 0:64], hb64, ALU.add, reads=["hb64"], writes=["g0"], banks=[0])
            tt("dve", g1_, g0_, g0_, ALU.mult, reads=["g0"], writes=["g1"])
            ts("dve", g1_, g1_, 0.044715, ALU.mult, 1.0, ALU.add, reads=["g1"], writes=["g1"])
            tt("dve", g1_, g1_, g0_, ALU.mult, reads=["g1", "g0"], writes=["g1"])
            act(g2_, g1_, AF.Tanh, scale=GC, reads=["g1"], writes=["g2"])
            stt(g3_, g2_, 1.0, g0_, ALU.add, ALU.mult, reads=["g2", "g0"], writes=["g3"])
            ts("dve", gkv[0][:, j0:j1 + 1], g3_[:, 0:nb], 0.5, ALU.mult, reads=["g3"], writes=["gk"])
            ts("dve", gkv[1][:, j0:j1 + 1], g3_[:, 32:32 + nb], 0.5, ALU.mult, reads=["g3"], writes=["gv"])

        def compress_st2(s_):
            j0 = max(0, 32 * s_ - 1)
            j1 = 32 * s_ + 30
            nb = j1 - j0 + 1
            mm(ps[0][0:64, 64:64 + nb], w2k_bf, gkv[0][:, j0:j1 + 1], True, True, 0, reads=[("w2", 0), ("w2", 1), "gk"])
            cp("dve", kcmpE[0:64, j0:j1 + 1], ps[0][0:64, 64:64 + nb], writes=["kcmpE"], banks=[0])
            for jc in range(j0 // 128, j1 // 128 + 1):
                mm(ps[0][:, 128 + 64 * jc:192 + 64 * jc], gkv[1][:, jc * 128:(jc + 1) * 128], w2v_bf, True, True, 0,
                   reads=[("w2", 0), ("w2", 1), "gv"])
                cp("dve", vaug[:, jc, 0:64], ps[0][:, 128 + 64 * jc:192 + 64 * jc], writes=[("vaug", jc)], banks=[0])

        def nsa_supertile(s_, R, bg=None, pace=1):
            LA = 4
            oc = ps[5][:, :].rearrange("p (h c) -> p h c", h=4)
            ow = ps[6][:, 0:260].rearrange("p (h c) -> p h c", h=4)
            osl = ps[7][:, 0:260].rearrange("p (h c) -> p h c", h=4)
            acc = {"cmp": (oc, 5), "win": (ow, 6), "sel": (osl, 7)}
            Rq = [("Rq", s_ % 2, h) for h in range(4)]
            jobs = []
            last_of = {}
            first_sel = {}
            for a in range(4):
                n = 4 * s_ + a
                ntot = 8 * n + 7
                for jc in range(2):
                    if ntot > 128 * jc:
                        jobs.append((n, a, "cmp", jc, min(128, ntot - 128 * jc)))
                        last_of[(n, "cmp")] = len(jobs) - 1
                for kc in range(max(0, n - 4), n + 1):
                    jobs.append((n, a, "win", kc, 128))
                    last_of[(n, "win")] = len(jobs) - 1
                first_sel[n] = len(jobs)
                for kc in range(0, n + 1):
                    jobs.append((n, a, "sel", kc, 128))
                    last_of[(n, "sel")] = len(jobs) - 1
            NJ = len(jobs)
            first = {}
            sbk = {}
            Ek = {}
            tkq = {}
            cmp_done = set()

            def qk(i):
                n, a, kind, c, nj = jobs[i]
                c0 = a * 128
                sb = SCB[scc[0] % 4]
                scc[0] += 1
                sbk[i] = sb
                if kind == "cmp":
                    lhs = kcmpE[:, c * 128:c * 128 + nj]
                    rd = ["kcmpE", "kcmpEc"]
                elif kind == "win":
                    lhs = KWE[:, c * 128:(c + 1) * 128]
                    rd = [("KWE", c // 4), "KWEc"]
                else:
                    lhs = KE[:, c * 128:(c + 1) * 128]
                    rd = [("KE", c // 4), "KEc"]
                mm(ps[sb][0:nj, 0:512], lhs, R[:, a, :, :].rearrange("p h q -> p (h q)"), True, True, sb, reads=rd + Rq + [("Rm", s_ % 2, a)])

            def ex(i):
                n, a, kind, c, nj = jobs[i]
                sb = sbk[i]
                E = Eb[ebc[0] % 4]
                ek = ("E", ebc[0] % 4)
                ebc[0] += 1
                Ek[i] = (E, ek)
                tm = tmpE[tmc[0] % 2]
                tk = ("tmpE", tmc[0] % 2)
                tmc[0] += 1
                psv = ps[sb][0:nj, 0:512].rearrange("p (h q) -> p h q", h=4)
                if kind == "cmp":
                    boundary = 128 * c + nj - 1 >= 8 * n - 1
                    bv = cbt_v[0:nj, :, c, n:n + 1]
                    bkey = "cbt"
                else:
                    boundary = c == n
                    bv = abt_v[0:nj, :, n - c:n - c + 1]
                    bkey = "abt"
                if kind != "cmp" and c == n:
                    tt("dve", tm, psv, dgt, ALU.add, reads=["dgt"], writes=[tk], banks=[sb])
                elif kind == "win" and c == n - 4:
                    tt("dve", tm, psv, fet, ALU.add, reads=["fet"], writes=[tk], banks=[sb])
                else:
                    tt("dve", tm[0:nj, :, :], psv, bv.to_broadcast([nj, 4, 128]), ALU.add, reads=[bkey], writes=[tk], banks=[sb])
                act(E[0:nj, :, :], tm[0:nj, :, :], AF.Exp, reads=[tk], writes=[ek])
                if kind == "cmp" and boundary:
                    asel(E[0:nj, :, :], E[0:nj, :, :], [[0, 4], [1, 128]], 128 * n - 31 - 2048 * c, -16, reads=[ek], writes=[ek])

            def pv(i):
                n, a, kind, c, nj = jobs[i]
                E, ek = Ek[i]
                ob, obank = acc[kind]
                for h in range(4):
                    st_ = not first.get((n, kind), False)
                    first[(n, kind)] = True
                    if kind == "cmp":
                        mm(ob[:, h, :], E[0:nj, h, :], vaug[0:nj, c, :], st_, False, obank, reads=[ek, ("vaug", c), "vaugc"])
                    elif kind == "win":
                        mm(ob[:, h, :], E[:, h, :], vwaug[:, c, :], st_, False, obank, reads=[ek, ("vw", c), "vw1"])
                    else:
                        mm(ob[:, h, :], E[:, h, :], vsaug[:, c, :], st_, False, obank, reads=[ek, ("vs", c), "vs1"])

            def topk_groups(n, a):
                c0 = a * 128
                p2 = n % 2
                rmk = ("Rm", s_ % 2, a)
                gv3 = gtn[s_ % 2][:, a, :].rearrange("p (h b) -> p h b", b=3)

                def tk0():
                    ts("dve", rz[p2][:, 0, :], oc[:, :, 64], 1e-37, ALU.max, writes=[("rz", p2, 0)], banks=[5])
                    ts("dve", imp[:, 0:63], oc[:, 0, 65:128], 1.0, ALU.mult, writes=["imp"], banks=[5])
                    recip(rz[p2][:, 0, :], rz[p2][:, 0, :], reads=[("rz", p2, 0)], writes=[("rz", p2, 0)])

                def tk1():
                    ts("dve", imp[:, 0:63], imp[:, 0:63], rz[p2][:, 0, 0:1], ALU.mult, reads=[("rz", p2, 0), "imp"], writes=["imp"])
                    for h in range(1, 4):
                        stt(imp[:, 0:63], oc[:, h, 65:128], rz[p2][:, 0, h:h + 1], imp[:, 0:63], ALU.mult, ALU.add,
                            reads=[("rz", p2, 0), "imp"], writes=["imp"], banks=[5])
                    stt(coef[p2][:, 0, :], gv3[:, :, 0], 1.0, rz[p2][:, 0, :], ALU.add, ALU.mult,
                        reads=[("gtn", s_ % 2, a), ("rz", p2, 0)], writes=[("coef", p2, 0)])

                def tk2():
                    if 2 * n + 1 < 64:
                        memset("dve", imp[0:64, 2 * n + 1:64], -BIG, writes=["imp"])
                    if 2 * n + 2 < 64:
                        memset("dve", imp[64:128, 2 * n + 2:64], -BIG, writes=["imp"])
                    memset("dve", imp[0:64, 2 * n:2 * n + 1], BIG, writes=["imp"])
                    memset("dve", imp[64:128, 2 * n + 1:2 * n + 2], BIG, writes=["imp"])
                    memset("dve", imp[:, 0:1], BIG, writes=["imp"])
                    S.op("dve", lambda e: e.max(out=m8[:, 0:8], in_=imp), reads=["imp"], writes=["m8a"])
                    for h in range(4):
                        ts("dve", yacc[p2][:, h, :], oc[:, h, 0:64], coef[p2][:, 0, h:h + 1], ALU.mult,
                           reads=[("coef", p2, 0)], writes=[("yacc", p2, h)], banks=[5])

                def tk3():
                    S.op("dve", lambda e: e.match_replace(out=rep, in_to_replace=m8[:, 0:8], in_values=imp, imm_value=-BIG),
                         reads=["imp", "m8a"], writes=["rep"])
                    S.op("dve", lambda e: e.max(out=m8[:, 8:16], in_=rep), reads=["rep"], writes=["m8b"])

                def tk4():
                    ts("dve", selw[:, 64:128], imp, m8[:, 15:16], ALU.is_ge, 1.0, ALU.subtract, reads=["imp", "m8b"], writes=["selw"])
                    tr(psb[0][:, 0:128], selw, identb, 0, reads=["selw", "selw0", "identb"])
                    for h in range(4):
                        stt(R[64:128, a, h, :], psb[0][64:128, 0:128], NEGM, rowt[64:128, h, :], ALU.mult, ALU.add,
                            reads=["rowt"], writes=[rmk], banks=[0])
                if n < 8:
                    def tk0s():
                        ts("dve", rz[p2][:, 0, :], oc[:, :, 64], 1e-37, ALU.max, writes=[("rz", p2, 0)], banks=[5])
                        recip(rz[p2][:, 0, :], rz[p2][:, 0, :], reads=[("rz", p2, 0)], writes=[("rz", p2, 0)])

                    def tk1s():
                        stt(coef[p2][:, 0, :], gv3[:, :, 0], 1.0, rz[p2][:, 0, :], ALU.add, ALU.mult,
                            reads=[("gtn", s_ % 2, a), ("rz", p2, 0)], writes=[("coef", p2, 0)])

                    def tk2s():
                        for h in range(4):
                            ts("dve", yacc[p2][:, h, :], oc[:, h, 0:64], coef[p2][:, 0, h:h + 1], ALU.mult,
                               reads=[("coef", p2, 0)], writes=[("yacc", p2, h)], banks=[5])
                    return [tk0s, tk1s, tk2s]
                return [tk0, tk1, tk2, tk3, tk4]

            def win_done(n, a):
                p2 = n % 2
                gv3 = gtn[s_ % 2][:, a, :].rearrange("p (h b) -> p h b", b=3)
                ts("dve", rz[p2][:, 2, :], ow[:, :, 64], 1e-37, ALU.max, writes=[("rz", p2, 2)], banks=[6])
                recip(rz[p2][:, 2, :], rz[p2][:, 2, :], reads=[("rz", p2, 2)], writes=[("rz", p2, 2)])
                stt(coef[p2][:, 2, :], gv3[:, :, 2], 1.0, rz[p2][:, 2, :], ALU.add, ALU.mult,
                    reads=[("gtn", s_ % 2, a), ("rz", p2, 2)], writes=[("coef", p2, 2)])
                for h in range(4):
                    stt(yacc[p2][:, h, :], ow[:, h, 0:64], coef[p2][:, 2, h:h + 1], yacc[p2][:, h, :], ALU.mult, ALU.add,
                        reads=[("coef", p2, 2), ("yacc", p2, h)], writes=[("yacc", p2, h)], banks=[6])

            def sel_done(n, a):
                p2 = n % 2
                gv3 = gtn[s_ % 2][:, a, :].rearrange("p (h b) -> p h b", b=3)
                ts("dve", rz[p2][:, 1, :], osl[:, :, 64], 1e-37, ALU.max, writes=[("rz", p2, 1)], banks=[7])
                recip(rz[p2][:, 1, :], rz[p2][:, 1, :], reads=[("rz", p2, 1)], writes=[("rz", p2, 1)])
                stt(coef[p2][:, 1, :], gv3[:, :, 1], 1.0, rz[p2][:, 1, :], ALU.add, ALU.mult,
                    reads=[("gtn", s_ % 2, a), ("rz", p2, 1)], writes=[("coef", p2, 1)])
                for h in range(4):
                    stt(yacc[p2][:, h, :], osl[:, h, 0:64], coef[p2][:, 1, h:h + 1], yacc[p2][:, h, :], ALU.mult, ALU.add,
                        reads=[("coef", p2, 1), ("yacc", p2, h)], writes=[("yacc", p2, h)], banks=[7])
                yb = yn[n % 2]
                stt(yb, yacc[p2][:, :, :].rearrange("p h c -> p (h c)"), 0.25, Zn[s_ % 2][:, a, :], ALU.mult, ALU.mult,
                    reads=[("yacc", p2, h) for h in range(4)] + [("Zn", s_ % 2, a)], writes=[("yn", n % 2)])

                def tail(n=n, yb=yb):
                    for dc in range(2):
                        tr(psb[0][:, 256 + dc * 128:256 + (dc + 1) * 128], yb[:, dc * 128:(dc + 1) * 128], identb, 0,
                           reads=[("yn", n % 2), "identb"])
                    cp("dve", ynT[n % 2], psb[0][:, 256:512].rearrange("p (c t) -> p c t", c=2), writes=[("ynT", n % 2)], banks=[0])
                    dma("pool", "y%d" % (n % 2), ynT_v[n // 16][:, :, (n % 16) * 128:(n % 16 + 1) * 128], ynT[n % 2],
                        reads=[("ynT", n % 2)], writes=[("ynT_d", n // 16)])
                pending_tail.append([2, tail])

            nq = [0]

            def emit_qk_upto(lim):
                while nq[0] < min(lim, NJ):
                    j = nq[0]
                    n, a, kind, c, nj = jobs[j]
                    if kind == "sel" and j == first_sel[n]:
                        if n not in cmp_done:
                            return
                        while tkq[n]:
                            tkq[n].pop(0)()
                    qk(j)
                    nq[0] += 1

            emit_qk_upto(LA)
            for i in range(NJ):
                n, a, kind, c, nj = jobs[i]
                ex(i)
                pv(i)
                if kind == "cmp" and i == last_of[(n, "cmp")]:
                    cmp_done.add(n)
                    tkq[n] = topk_groups(n, a)
                for nn in list(tkq.keys()):
                    if tkq[nn]:
                        tkq[nn].pop(0)()
                if kind == "win" and i == last_of[(n, "win")]:
                    win_done(n, a)
                if kind == "sel" and i == last_of[(n, "sel")]:
                    while tkq[n]:
                        tkq[n].pop(0)()
                    sel_done(n, a)
                for pt in list(pending_tail):
                    pt[0] -= 1
                    if pt[0] <= 0:
                        pending_tail.remove(pt)
                        pt[1]()
                emit_qk_upto(i + 1 + LA)
                if bg is not None:
                    bgc[0] += 1
                    if bgc[0] % pace == 0:
                        next(bg, None)
            for nn in list(tkq.keys()):
                while tkq[nn]:
                    tkq[nn].pop(0)()
            for pt in list(pending_tail):
                pending_tail.remove(pt)
                pt[1]()


        def FEN_gen(s_):
            for a in range(4):
                for _ in front_end(4 * s_ + a, a, xs, xg, junk, hTn[s_ % 2], ("hTn", s_ % 2), 32 + 4 * s_ + a, gen=True):
                    yield

        def INPROJ_rest(s_):
            R = Rb[s_ % 2]
            T0 = s_ * 512
            p2 = s_ % 2
            hT = hTn[p2]
            hk = ("hTn", p2)
            HT_ALL = [(hk, a, r) for a in range(4) for r in range(2)]
            for oi, ci in enumerate((4, 5, 0, 1, 2, 3, 6, 7)):
                c0, M = FMC[ci]
                bk = 1 + (oi % 2)
                if oi == 2:
                    compress_st(s_)
                for k in range(16):
                    mm(ps[bk][0:M, :], wn_bf[:, k, c0:c0 + M], hT[:, k, :], k == 0, k == 15, bk,
                       reads=HT_ALL + [("wn", k // 4)])
                if ci < 4:
                    act(R[0:64, :, ci, :], ps[bk][0:64, :].rearrange("p (a q) -> p a q", a=4), AF.Identity, bias=bq8[0:64, ci:ci + 1], scale=0.125,
                        reads=["bq8"], writes=[("Rq", s_ % 2, ci)], banks=[bk])
                elif ci in (4, 5):
                    kv = ci - 4
                    act(c2T[kv][0:64, T0 + 1:T0 + 513], ps[bk][0:64, :], AF.Identity, bias=biasFMn[0:64, ci:ci + 1],
                        reads=["biasFMn"], writes=[("c2T", kv, s_)], banks=[bk])
                    ts("dve", c2T[kv][64:128, T0:T0 + 512], ps[bk][64:128, :], biasFMn[64:128, ci:ci + 1], ALU.add,
                       reads=["biasFMn"], writes=[("c2T", kv, s_)], banks=[bk])
                elif ci == 6:
                    ts("dve", KE[0:64, T0:T0 + 512], ps[bk][0:64, :], biasFMn[0:64, ci:ci + 1], ALU.add,
                       reads=["biasFMn"], writes=[("KE", s_)], banks=[bk])
                else:
                    act(KWE[0:64, T0:T0 + 512], ps[bk][0:64, :], AF.Identity, bias=biasFMn[0:64, ci:ci + 1],
                        reads=["biasFMn"], writes=[("KWE", s_)], banks=[bk])
            for a in range(4):
                n = 4 * s_ + a
                bk = 1 + (a % 2)
                for k in range(16):
                    mm(ps[bk][:, 0:396], hT[:, k, a * 128:(a + 1) * 128], wn_bf[:, k, 640:1036], k == 0, k == 15, bk,
                       reads=[(hk, a, 0), (hk, a, 1), ("wn", k // 4)])
                tt("dve", vsaug[:, n, 0:64], ps[bk][:, 0:64], biasTMn[:, 0:64], ALU.add, reads=["biasTMn"], writes=[("vs", n)], banks=[bk])
                tt("dve", vwaug[:, n, 0:64], ps[bk][:, 64:128], biasTMn[:, 64:128], ALU.add, reads=["biasTMn"], writes=[("vw", n)], banks=[bk])
                tt("dve", tg[a % 2], ps[bk][:, 128:140], biasTMn[:, 128:140], ALU.add, reads=["biasTMn"], writes=[("tg", a % 2)], banks=[bk])
                tt("dve", tzn[a % 2], ps[bk][:, 140:396], biasTMn[:, 140:396], ALU.add, reads=["biasTMn"], writes=[("tzn", a % 2)], banks=[bk])
                act(gtn[p2][:, a, :], tg[a % 2], AF.Tanh, scale=0.5, reads=[("tg", a % 2)], writes=[("gtn", p2, a)])
                act(thzn[a % 2], tzn[a % 2], AF.Tanh, scale=0.5, reads=[("tzn", a % 2)], writes=[("thzn", a % 2)])
                stt(Zn[p2][:, a, :], thzn[a % 2], 1.0, tzn[a % 2], ALU.add, ALU.mult, reads=[("thzn", a % 2), ("tzn", a % 2)], writes=[("Zn", p2, a)])

        def count_steps(g):
            return sum(1 for _ in g)

        if ntiles:
            run_interleaved(FEN_gen(0))
        for s_ in range(ntiles):
            R = Rb[s_ % 2]
            if mtiles == NST and ntiles == NST:
                if s_ == 1:
                    S.collective(lambda e: e.collective_compute("AllGather", ALU.bypass, replica_groups=GROUPS,
                                                                ins=[ymT_d[0].ap().opt()], outs=[ym_all[0].ap().opt()]),
                                 reads=[("ymT_d", 0)], writes=[("ym_all", 0)])
                elif s_ == 2:
                    S.collective(lambda e: e.collective_compute("AllGather", ALU.bypass, replica_groups=GROUPS,
                                                                ins=[ymT_d[1].ap().opt()], outs=[ym_all[1].ap().opt()]),
                                 reads=[("ymT_d", 1)], writes=[("ym_all", 1)])
                elif s_ == 4:
                    while pending_tail:
                        pending_tail.pop(0)[1]()
                    S.collective(lambda e: e.collective_compute("AllGather", ALU.bypass, replica_groups=GROUPS,
                                                                ins=[ynT_d[0].ap().opt()], outs=[yn_all[0].ap().opt()]),
                                 reads=[("ynT_d", 0)], writes=[("yn_all", 0)])
            INPROJ_rest(s_)
            compress_st2(s_)
            bg = FEN_gen(s_ + 1) if s_ + 1 < ntiles else None
            nchunks = sum((2 if 8 * n + 7 > 128 else 1) + (n - max(0, n - 4) + 1) + (n + 1) for n in range(4 * s_, 4 * s_ + 4))
            nsa_supertile(s_, R, bg, max(1, nchunks // 22))
            if bg is not None:
                for _ in bg:
                    pass
        if debug:
            for hf in range(2):
                if ntiles > 4 * hf:
                    w_ = min(2048, 512 * ntiles - 2048 * hf)
                    dma("sp", "c5", dbg["yn"][:, 2048 * hf:2048 * hf + w_], ynT_d[hf].ap()[:, 0:w_], reads=[("ynT_d", hf)], writes=["dbg_yn"])
        if upto == 2:
            return finish(["dbg_ym", "dbg_yn", "dbg_mod"])
        S.barrier()
        off[0] = PM0

        wo_bf = alloc((16, 512), BF16)
        yT = [alloc((16, 512), BF16) for _ in range(2)]
        xr = [alloc((512,), F32) for _ in range(2)]
        hres = alloc((NT, 512), F32)
        fg_bc = alloc((512,), F32)
        ssp = alloc((32,), F32)
        sq4 = alloc((4, 32), F32)
        rs_f = alloc((32,), F32)
        ot = [alloc((512,), F32) for _ in range(4)]
        junkf = alloc((512,), BF16)
        assert off[0] <= AW
        wo_v = I["wout"].rearrange("(k p) n -> p k n", p=128)
        for q4 in range(4):
            dma("pool", "w%d" % q4, wo_bf[:, 4 * q4:4 * q4 + 4, :], wo_v[:, 4 * q4:4 * q4 + 4, :], writes=[("wo", q4)])
        dma("sp", "c0", fg_bc, I["fg"].partition_broadcast(128), writes=["fg_bc"])
        S.collective(lambda e: e.collective_compute("AllGather", ALU.bypass, replica_groups=GROUPS,
                                                    ins=[ynT_d[1].ap().opt()], outs=[yn_all[1].ap().opt()]),
                     reads=[("ynT_d", 1)], writes=[("yn_all", 1)])
        yall_m = [t_.ap().rearrange("(k p) t -> p k t", p=128) for t_ in ym_all]
        yall_n = [t_.ap().rearrange("(k p) t -> p k t", p=128) for t_ in yn_all]
        for s_ in range(NST):
            yb = yT[s_ % 2]
            dma("sp", "ya%d" % (s_ % 2), yb[:, 0:8, :], yall_m[s_ // 4][:, :, (s_ % 4) * 512:(s_ % 4 + 1) * 512], reads=[("ym_all", s_ // 4)], writes=[("yT", s_ % 2, 0)])
            dma("sp", "yb%d" % (s_ % 2), yb[:, 8:16, :], yall_n[s_ // 4][:, :, (s_ % 4) * 512:(s_ % 4 + 1) * 512], reads=[("yn_all", s_ // 4)], writes=[("yT", s_ % 2, 1)])
            for a in range(4):
                n = 4 * s_ + a
                bk = 1 + (n % 2)
                dma("sp", "x%d" % (n % 2), xr[n % 2], I["xcol"][n * 128:(n + 1) * 128, :], writes=[("xr", n % 2)])
                for k in range(16):
                    mm(ps[bk][:, :], yb[:, k, a * 128:(a + 1) * 128], wo_bf[:, k, :], k == 0, k == 15, bk,
                       reads=[("yT", s_ % 2, k // 8), ("wo", k // 4)])
                tt("dve", hres[:, n, :], ps[bk][:, :], gate_bc, ALU.mult, reads=["gate_bc"], writes=[("hres", n)], banks=[bk])
                tt("dve", hres[:, n, :], hres[:, n, :], xr[n % 2], ALU.add, reads=[("hres", n), ("xr", n % 2)], writes=[("hres", n)])
                act(junkf, hres[:, n, :], AF.Square, accum=ssp[:, n:n + 1], reads=[("hres", n)], writes=[("ssp", n)])
                tt("dve", hres[:, n, :], hres[:, n, :], fg_bc, ALU.mult, reads=[("hres", n), ("ssp", n), "fg_bc"], writes=[("hres", n)])
        dma("sp", "c1", sq_in.ap(), ssp, reads=[("ssp", n) for n in range(NT)], writes=["sq_in"])
        S.collective(lambda e: e.collective_compute("AllGather", ALU.bypass, replica_groups=GROUPS,
                                                    ins=[sq_in.ap().opt()], outs=[sq_all.ap().opt()]),
                     reads=["sq_in"], writes=["sq_all"])
        dma("sp", "c2", sq4, sq_all.ap().rearrange("(r p) n -> p r n", p=128), reads=["sq_all"], writes=["sq4"])
        tt("dve", rs_f, sq4[:, 0, :], sq4[:, 1, :], ALU.add, reads=["sq4"], writes=["rs_f"])
        tt("dve", rs_f, rs_f, sq4[:, 2, :], ALU.add, reads=["sq4", "rs_f"], writes=["rs_f"])
        tt("dve", rs_f, rs_f, sq4[:, 3, :], ALU.add, reads=["sq4", "rs_f"], writes=["rs_f"])
        ts("dve", rs_f, rs_f, 1.0 / D, ALU.mult, EPS, ALU.add, reads=["rs_f"], writes=["rs_f"])
        tt("pool", rs_f, rs_f, neghalf, ALU.pow, reads=["rs_f", "neghalf"], writes=["rs_f"])
        for n in range(NT):
            eng3 = ("dve", "act")[n % 2]
            if eng3 == "act":
                act(ot[n % 4], hres[:, n, :], AF.Copy, scale=rs_f[:, n:n + 1], reads=[("hres", n), "rs_f"], writes=[("ot", n % 4)])
            else:
                ts(eng3, ot[n % 4], hres[:, n, :], rs_f[:, n:n + 1], ALU.mult, reads=[("hres", n), "rs_f"], writes=[("ot", n % 4)])
            dma("sp", "o%d" % (n % 4), out_d[n * 128:(n + 1) * 128, :], ot[n % 4], reads=[("ot", n % 4)], writes=[("out", n)])
        return finish([("out", n) for n in range(NT)] + (["dbg_ym", "dbg_yn", "dbg_mod"] if debug else []))


def _consts(j):
    slopes = np.array([2.0 ** (-8.0 * (4 * j + hl + 1) / 16) for hl in range(4)], np.float64)
    p = np.arange(128)
    expand = np.zeros((64, T), np.float32)
    expand[np.arange(T) // 64, np.arange(T)] = 1.0
    e0 = np.zeros((64, T), np.float32)
    e0[0, :] = 1.0
    rowt = np.zeros((64, 4, 128), np.float32)
    for hl in range(4):
        rowt[:, hl, :] = (-slopes[hl] * np.arange(128))[None, :]
    SHIFT = 30.0
    abt = np.zeros((128, 4, 32), np.float64)
    for hl in range(4):
        for rel in range(32):
            abt[:, hl, rel] = slopes[hl] * (p - 128.0 * rel) - SHIFT
    cbt = np.zeros((128, 4, 2, 32), np.float64)
    for hl in range(4):
        for jc in range(2):
            for n in range(32):
                cbt[:, hl, jc, n] = slopes[hl] * (16.0 * (128 * jc + p) + 15.5 - 128.0 * n) - SHIFT
    qq = np.arange(128)
    dgt = np.zeros((128, 4, 128), np.float64)
    fet = np.zeros((128, 4, 128), np.float64)
    for hl in range(4):
        dgt[:, hl, :] = (slopes[hl] * p - SHIFT)[:, None] + np.where(p[:, None] > qq[None, :], -30000.0, 0.0)
        fet[:, hl, :] = (slopes[hl] * (p - 512.0) - SHIFT)[:, None] + np.where(p[:, None] <= qq[None, :], -30000.0, 0.0)
    ovl1 = np.zeros((128, 2, 64), np.float32)
    for jc in range(2):
        for pp in range(128):
            jj = 128 * jc + pp
            ovl1[pp, jc, 0] = 1.0
            for s in range(63):
                if 16 * jj <= 64 * s + 63 and 16 * jj + 31 >= 64 * s:
                    ovl1[pp, jc, 1 + s] = 1.0
    return dict(
        identb=np.eye(128, dtype=np.float32).astype(bf), identf=np.eye(128, dtype=np.float32),
        triu=np.triu(np.ones((128, 128), np.float32)), onesf=np.ones((128, 128), np.float32),
        expand=expand.astype(bf), e0=e0.astype(bf), rowt=rowt.reshape(64, 512).astype(np.float32),
        abt=abt.reshape(128, 128).astype(np.float32), cbt=cbt.reshape(128, 256).astype(np.float32),
        ovl1=ovl1.reshape(128, 128).astype(bf), dgt=dgt.reshape(128, 512).astype(np.float32),
        fet=fet.reshape(128, 512).astype(np.float32))


def make_in_maps(x, c, ln_g, w_ada, b_ada, w_in, b_in, m_conv_w, m_conv_b, m_wq, m_wk, m_norm_w,
                 m_skip, m_f_bias, n_pos_k, n_pos_v, n_w1_k, n_w2_k, n_w1_v, n_w2_v, w_out, final_g):
    f = lambda v: np.ascontiguousarray(np.asarray(v, dtype=np.float32))
    x, c, ln_g, w_ada, b_ada, w_in, b_in = map(f, (x, c, ln_g, w_ada, b_ada, w_in, b_in))
    m_conv_w, m_conv_b, m_wq, m_wk, m_norm_w, m_skip, m_f_bias = map(f, (m_conv_w, m_conv_b, m_wq, m_wk, m_norm_w, m_skip, m_f_bias))
    n_pos_k, n_pos_v, n_w1_k, n_w2_k, n_w1_v, n_w2_v, w_out, final_g = map(f, (n_pos_k, n_pos_v, n_w1_k, n_w2_k, n_w1_v, n_w2_v, w_out, final_g))
    W = w_in[0]
    B = b_in[0]
    O = dict(mx=0, mv=1024, mo=2048, mz=3072, mi=4096, mf=4100, nq=4104, nkc=5128, nvc=5384, nks=5640,
             nvs=5896, nkw=6152, nvw=6408, ng=6664, nz=6712)
    maps = []
    for r in range(8):
        b, j = divmod(r, 4)
        r256 = np.arange(256 * j, 256 * j + 256)
        r64 = np.arange(64 * j, 64 * j + 64)
        cm = np.concatenate([O["mx"] + r256, O["mv"] + r256, O["mo"] + r256, O["mz"] + r256, [O["mi"] + j], [O["mf"] + j]])
        cn_fm = np.concatenate([O["nq"] + r256, O["nkc"] + r64, O["nkc"] + r64, O["nvc"] + r64, O["nvc"] + r64,
                                O["nks"] + r64, O["nkw"] + r64])
        cn_tm = np.concatenate([O["nvs"] + r64, O["nvw"] + r64, O["ng"] + np.arange(12 * j, 12 * j + 12), O["nz"] + r256])
        cn = np.concatenate([cn_fm, cn_tm])
        bn_fm = np.zeros((128, 8), np.float32)
        for h in range(4):
            bn_fm[0:64, h] = B[O["nq"] + 256 * j + 64 * h + np.arange(64)]
        bn_fm[0:64, 4] = B[O["nkc"] + r64]; bn_fm[64:128, 4] = B[O["nkc"] + r64]
        bn_fm[0:64, 5] = B[O["nvc"] + r64]; bn_fm[64:128, 5] = B[O["nvc"] + r64]
        bn_fm[0:64, 6] = B[O["nks"] + r64]
        bn_fm[0:64, 7] = B[O["nkw"] + r64]
        wrows = np.concatenate([np.concatenate([np.arange(256 * jj, 256 * jj + 256) for jj in range(4)]),
                                np.concatenate([1024 + np.arange(256 * jj, 256 * jj + 256) for jj in range(4)])])
        pos2 = lambda pz: np.ascontiguousarray(pz.reshape(16, 2, 64).transpose(1, 2, 0).reshape(128, 16))
        m = dict(
            x=x[b], xcol=np.ascontiguousarray(x[b][:, 512 * j:512 * j + 512]),
            c_col=np.ascontiguousarray(c[b].reshape(16, 128).T),
            wada=np.ascontiguousarray(np.concatenate([w_ada[0][:, 1024 * j:1024 * j + 1024],
                                                      w_ada[0][:, 4096 + 512 * j:4096 + 512 * j + 512]], axis=1)),
            bada=np.concatenate([b_ada[0][1024 * j:1024 * j + 1024], b_ada[0][4096 + 512 * j:4096 + 512 * j + 512]])[None, :],
            lng=ln_g[0][None, :],
            wm=np.ascontiguousarray(W[:, cm]),
            bm_fm=np.ascontiguousarray(B[O["mx"] + r256].reshape(2, 128).T),
            bm_tm=B[cm[256:]][None, :],
            wn=np.ascontiguousarray(W[:, cn]), bn_fm=bn_fm, bn_tm=B[cn_tm][None, :],
            convw=np.ascontiguousarray(m_conv_w[0][:, r256].reshape(4, 2, 128).transpose(2, 1, 0).reshape(128, 8)),
            convb=np.ascontiguousarray(m_conv_b[0][r256].reshape(2, 128).T),
            wq=m_wq[0, j], wk=m_wk[0, j],
            normw=m_norm_w[0][r256][None, :], skip=m_skip[0][r256][None, :],
            fbias=m_f_bias[0][j:j + 1][None, :],
            posk2=pos2(n_pos_k[0]), posv2=pos2(n_pos_v[0]),
            w1k=n_w1_k[0], w1v=n_w1_v[0], w2k=n_w2_k[0], w2v=n_w2_v[0],
            wout=np.ascontiguousarray(w_out[0][wrows][:, 512 * j:512 * j + 512]),
            fg=final_g[512 * j:512 * j + 512][None, :],
        )
        m.update(_consts(j))
        m = {k: np.ascontiguousarray(v) for k, v in m.items()}
        maps.append(m)
    return maps


_NC_CACHE = {}


def kernel(**inputs):
    maps = make_in_maps(**inputs)
    if "nc" not in _NC_CACHE:
        _NC_CACHE["nc"] = build(False)
    nc = _NC_CACHE["nc"]
    res = run_bass_kernel_spmd(nc, maps, core_ids=list(range(8)))
    out = np.zeros((2, T, D), np.float32)
    for r in range(8):
        b, j = divmod(r, 4)
        out[b, :, 512 * j:512 * j + 512] = res.results[r]["out"]
    return out
```

```python
import math
from contextlib import ExitStack

import numpy as np
import ml_dtypes

import concourse.bass as bass
import concourse.mybir as mybir
from concourse.bass_utils import run_bass_kernel_spmd

F32 = mybir.dt.float32
BF16 = mybir.dt.bfloat16
AF = mybir.ActivationFunctionType
ALU = mybir.AluOpType
bf = ml_dtypes.bfloat16

T = 4096
D = 2048
NT = 32
NST = 8
EPS = 1e-6
BIG = 1.0e30
NEGM = 30000.0
GC = 0.7978845608028654

ENG = ("pe", "act", "dve", "pool", "sp")


class Sched:
    def __init__(self, nc):
        self.nc = nc
        self.items = {e: [] for e in ENG}
        self.count = {e: 0 for e in ENG}
        self.known = {e: {} for e in ENG}
        self.clock = {}
        self.last_w = {}
        self.readers = {}
        self.bank_last = {}
        self.dma_count = {}
        self.ncc = 0
        self.sems = {}

    def _need(self, eng, ev, waits):
        if ev is None:
            return
        k, v = ev
        if k == "pe" and eng == "pe":
            return
        if self.known[eng].get(k, 0) >= v:
            return
        waits[k] = max(waits.get(k, 0), v)
        for kk, vv in self.clock.get(ev, {}).items():
            if self.known[eng].get(kk, 0) < vv:
                self.known[eng][kk] = vv
        self.known[eng][k] = max(self.known[eng].get(k, 0), v)

    def _deps(self, eng, reads, writes, banks):
        waits = {}
        for b in reads:
            self._need(eng, self.last_w.get(b), waits)
        for b in writes:
            self._need(eng, self.last_w.get(b), waits)
            for ev in self.readers.get(b, ()):
                self._need(eng, ev, waits)
        for b in banks:
            for e2, ev in self.bank_last.get(b, {}).items():
                if e2 != eng:
                    self._need(eng, ev, waits)
        return waits

    def _commit(self, eng, ev, reads, writes, banks):
        clk = dict(self.known[eng])
        clk[ev[0]] = max(clk.get(ev[0], 0), ev[1])
        self.clock[ev] = clk
        for b in reads:
            self.readers.setdefault(b, []).append(ev)
        for b in writes:
            self.last_w[b] = ev
            self.readers[b] = []
        for b in banks:
            self.bank_last.setdefault(b, {})[eng] = ev

    def op(self, eng, fn, reads=(), writes=(), banks=()):
        waits = self._deps(eng, reads, writes, banks)
        self.count[eng] += 1
        ev = (eng, self.count[eng])
        self.items[eng].append((waits, fn, (eng, 1)))
        self._commit(eng, ev, reads, writes, banks)
        return ev

    def dma(self, q, slot, fn, reads=(), writes=()):
        waits = self._deps(q, reads, writes, ())
        prev = self.dma_count.get(slot, 0)
        key = ("dma", slot)
        if prev:
            self._need(q, (key, prev), waits)
        self.dma_count[slot] = prev + 16
        ev = (key, prev + 16)
        self.items[q].append((waits, fn, (key, 16)))
        self._commit(q, ev, reads, writes, ())
        return ev

    def collective(self, fn, reads=(), writes=()):
        waits = self._deps("pool", reads, writes, ())
        self.ncc += 1
        key = ("cc", self.ncc)
        ev = (key, 1)
        self.items["pool"].append((waits, fn, (key, None)))
        self._commit("pool", ev, reads, writes, ())
        w2 = {}
        self._need("pool", ev, w2)
        self.items["pool"].append((w2, None, None))
        return ev

    def barrier(self):
        evs = [(e, self.count[e]) for e in ENG if self.count[e]]
        evs += [(("dma", s), c) for s, c in self.dma_count.items()]
        evs += [(("cc", i + 1), 1) for i in range(self.ncc)]
        for e in ENG:
            waits = {}
            for ev in evs:
                if ev[0] == e:
                    continue
                self._need(e, ev, waits)
            if waits:
                self.items[e].append((waits, None, None))

    def wait_all(self, eng, bufs):
        waits = {}
        for b in bufs:
            self._need(eng, self.last_w.get(b), waits)
        self.items[eng].append((waits, None, None))

    def emit(self, stack):
        nc = self.nc
        keys = set()
        for e in ENG:
            for waits, fn, inc in self.items[e]:
                keys.update(waits.keys())
                if inc:
                    keys.add(inc[0])
        for i, k in enumerate(sorted(keys, key=str)):
            self.sems[k] = stack.enter_context(nc.semaphore("s%d" % i))
        block = stack.enter_context(nc.Block())
        starters = {"pe": block.tensor, "act": block.scalar, "dve": block.vector,
                    "pool": block.gpsimd, "sp": block.sync}

        def make(e):
            def body(engobj):
                for waits, fn, inc in self.items[e]:
                    for k, v in waits.items():
                        engobj.wait_ge(self.sems[k], v)
                    if fn is not None:
                        ins = fn(engobj)
                        if inc[1] is None:
                            ins.then_inc(self.sems[inc[0]])
                        else:
                            ins.then_inc(self.sems[inc[0]], inc[1])
            return body

        for e in ENG:
            if self.items[e]:
                starters[e](make(e))
        return len(keys)


INPUT_SPECS = [
    ("x", [T, D], F32), ("xcol", [T, 512], F32), ("c_col", [128, 16], F32),
    ("wada", [D, 1536], F32), ("bada", [1, 1536], F32), ("lng", [1, D], F32),
    ("wm", [D, 1026], F32), ("bm_fm", [128, 2], F32), ("bm_tm", [1, 770], F32),
    ("wn", [D, 1036], F32), ("bn_fm", [128, 8], F32), ("bn_tm", [1, 396], F32),
    ("convw", [128, 8], F32), ("convb", [128, 2], F32),
    ("wq", [256, 256], F32), ("wk", [256, 256], F32),
    ("normw", [1, 256], F32), ("skip", [1, 256], F32), ("fbias", [1, 1], F32),
    ("posk2", [128, 16], F32), ("posv2", [128, 16], F32),
    ("w1k", [D, 128], F32), ("w1v", [D, 128], F32), ("w2k", [128, 64], F32), ("w2v", [128, 64], F32),
    ("wout", [D, 512], F32), ("fg", [1, 512], F32),
    ("identb", [128, 128], BF16), ("identf", [128, 128], F32), ("triu", [128, 128], F32),
    ("onesf", [128, 128], F32), ("expand", [64, T], BF16), ("e0", [64, T], BF16),
    ("rowt", [64, 512], F32), ("abt", [128, 128], F32), ("cbt", [128, 256], F32),
    ("ovl1", [128, 128], BF16), ("dgt", [128, 512], F32), ("fet", [128, 512], F32),
]


def build(debug=False, upto=3, mtiles=NST, ntiles=NST):
    nc = bass.Bass("TRN2", target_bir_lowering=False)
    I = {}
    for name, shape, dt in INPUT_SPECS:
        I[name] = nc.dram_tensor(name, shape, dt, kind="ExternalInput").ap()
    out_d = nc.dram_tensor("out", [T, 512], F32, kind="ExternalOutput").ap()
    dbg = {}
    if debug:
        dbg["ym"] = nc.dram_tensor("dbg_ym", [256, T], BF16, kind="ExternalOutput").ap()
        dbg["yn"] = nc.dram_tensor("dbg_yn", [256, T], BF16, kind="ExternalOutput").ap()
        dbg["mod"] = nc.dram_tensor("dbg_mod", [4, 1024], F32, kind="ExternalOutput").ap()
    agin = nc.dram_tensor("agin", [1, 1024], F32)
    agout = nc.dram_tensor("agout", [4, 1024], F32)
    gate_d = nc.dram_tensor("gate_d", [1, 512], F32)
    ymT_d = [nc.dram_tensor("ymT_d%d" % i, [256, T // 2], BF16) for i in range(2)]
    ym_all = [nc.dram_tensor("ym_all%d" % i, [1024, T // 2], BF16) for i in range(2)]
    ynT_d = [nc.dram_tensor("ynT_d%d" % i, [256, T // 2], BF16) for i in range(2)]
    yn_all = [nc.dram_tensor("yn_all%d" % i, [1024, T // 2], BF16) for i in range(2)]
    sq_in = nc.dram_tensor("sq_in", [128, 32], F32)
    sq_all = nc.dram_tensor("sq_all", [512, 32], F32)
    GROUPS = [[0, 1, 2, 3], [4, 5, 6, 7]]

    with ExitStack() as st:
        AW = 53200
        arena = st.enter_context(nc.sbuf_tensor("arena", [128, AW], F32))
        ps = [st.enter_context(nc.psum_tensor("ps%d" % i, [128, 512], F32)) for i in range(8)]
        psb = [p[:, :].bitcast(BF16) for p in ps]
        S = Sched(nc)
        off = [0]

        def finish(bufs):
            S.wait_all("sp", bufs)
            nsem = S.emit(st)
            build.stats = dict(nsem=nsem, counts=dict(S.count), dmas=len(S.dma_count))
            return nc

        def alloc(shape, dt):
            n = int(np.prod(shape))
            words = n if dt == F32 else (n + 1) // 2
            assert off[0] + words <= AW, ("arena overflow", off[0], words)
            ap = arena[:, off[0]:off[0] + words]
            off[0] += words
            if dt != F32:
                ap = ap.bitcast(dt)
                if 2 * words != n:
                    ap = ap[:, 0:n]
            if len(shape) == 2:
                ap = ap.rearrange("p (a b) -> p a b", a=shape[0])
            elif len(shape) == 3:
                ap = ap.rearrange("p (a b c) -> p a b c", a=shape[0], b=shape[1])
            return ap

        def alloc_at(offset, shape, dt):
            save = off[0]
            off[0] = offset
            ap = alloc(shape, dt)
            end = off[0]
            off[0] = save
            return ap, end

        def mm(out, lhsT, rhs, start, stop, bank, reads=()):
            S.op("pe", lambda e: e.matmul(out=out, lhsT=lhsT, rhs=rhs, start=start, stop=stop,
                                          skip_group_check=True), reads=reads, banks=[bank])

        def tr(out, in_, ident, bank, reads=()):
            S.op("pe", lambda e: e.transpose(out=out, in_=in_, identity=ident), reads=reads, banks=[bank])

        def act(out, in_, func, bias=None, scale=None, accum=None, reads=(), writes=(), banks=()):
            kw = {}
            if bias is not None:
                kw["bias"] = bias
            if scale is not None:
                kw["scale"] = scale
            if accum is not None:
                kw["accum_out"] = accum
            S.op("act", lambda e: e.activation(out=out, in_=in_, func=func, **kw),
                 reads=reads, writes=writes, banks=banks)

        def tt(eng, out, in0, in1, op, reads=(), writes=(), banks=()):
            S.op(eng, lambda e: e.tensor_tensor(out=out, in0=in0, in1=in1, op=op), reads=reads, writes=writes, banks=banks)

        def ts(eng, out, in0, s1, op0, s2=None, op1=None, reads=(), writes=(), banks=()):
            if op1 is None:
                S.op(eng, lambda e: e.tensor_scalar(out=out, in0=in0, scalar1=s1, scalar2=None, op0=op0),
                     reads=reads, writes=writes, banks=banks)
            else:
                S.op(eng, lambda e: e.tensor_scalar(out=out, in0=in0, scalar1=s1, scalar2=s2, op0=op0, op1=op1),
                     reads=reads, writes=writes, banks=banks)

        def stt(out, in0, scalar, in1, op0, op1, reads=(), writes=(), banks=()):
            S.op("dve", lambda e: e.scalar_tensor_tensor(out=out, in0=in0, scalar=scalar, in1=in1, op0=op0, op1=op1),
                 reads=reads, writes=writes, banks=banks)

        def cp(eng, out, in_, reads=(), writes=(), banks=()):
            S.op(eng, lambda e: e.tensor_copy(out=out, in_=in_), reads=reads, writes=writes, banks=banks)

        def memset(eng, out, val, writes=()):
            S.op(eng, lambda e: e.memset(out, val), writes=writes)

        def recip(out, in_, reads=(), writes=()):
            S.op("dve", lambda e: e.reciprocal(out=out, in_=in_), reads=reads, writes=writes)

        def dma(q, slot, out, in_, reads=(), writes=()):
            S.dma(q, slot, lambda e: e.dma_start(out=out, in_=in_), reads=reads, writes=writes)

        def asel(out, in_, pattern, base, cm, reads=(), writes=()):
            S.op("pool", lambda e: e.affine_select(out=out, in_=in_, pattern=pattern, compare_op=ALU.is_ge,
                                                   fill=0.0, base=base, channel_multiplier=cm),
                 reads=reads, writes=writes)

        identb = alloc((128,), BF16)
        identf = alloc((128,), F32)
        triu = alloc((128,), F32)
        onesf = alloc((128,), F32)
        gs_bc = alloc((D,), F32)
        shift_col = alloc((16,), F32)
        shift_bf = alloc((16,), BF16)
        shiftrep = alloc((16, 128), BF16)
        gate_bc = alloc((512,), F32)
        neghalf = alloc((32,), F32)
        ss = alloc((64,), F32)
        rstd = alloc((64,), F32)
        dma("sp", "c0", identb, I["identb"], writes=["identb"])
        dma("sp", "c1", identf, I["identf"], writes=["identf"])
        dma("sp", "c2", triu, I["triu"], writes=["triu"])
        dma("sp", "c3", onesf, I["onesf"], writes=["onesf"])
        memset("dve", neghalf, -0.5, writes=["neghalf"])
        wn_bf = alloc((16, 1036), BF16)
        w1 = [alloc((16, 128), BF16) for _ in range(2)]
        w2k_bf = alloc((64,), BF16)
        w2v_bf = alloc((64,), BF16)
        pos2 = [alloc((16,), BF16) for _ in range(2)]
        G0 = off[0]
        WTOP = AW - (8 * 1026 + 2 * 256 + 16)
        wm_bf, e_ = alloc_at(WTOP, (16, 1026), BF16)
        wq_bf, e_ = alloc_at(e_, (2, 256), BF16)
        wk_bf, e_ = alloc_at(e_, (2, 256), BF16)
        assert e_ <= AW

        c_col = alloc((16,), F32)
        th16 = alloc((16,), F32)
        sc_f = alloc((16,), F32)
        bada = alloc((1536,), F32)
        mod_sb = alloc((1536,), F32)
        lng_bc = alloc((D,), F32)
        scale_bc = alloc((D,), F32)
        sh16 = alloc((128,), F32)
        wst = [alloc((4, 1536), F32) for _ in range(2)]
        dma("sp", "c0", c_col, I["c_col"], writes=["c_col"])
        wada_v = I["wada"].rearrange("(k p) n -> p k n", p=128)
        for q4 in range(2):
            dma("sp", "w%d" % q4, wst[q4][:, :, 0:1024], wada_v[:, 4 * q4:4 * q4 + 4, 0:1024], writes=[("wst", q4)])
        dma("sp", "c1", bada[0:1, :], I["bada"], writes=["bada"])
        dma("sp", "c2", lng_bc, I["lng"].partition_broadcast(128), writes=["lng_bc"])
        assert off[0] <= WTOP, ("phase-0 scratch overlaps preloaded weights", off[0], WTOP)
        wm_v = I["wm"].rearrange("(k p) n -> p k n", p=128)
        for q4 in range(4):
            dma("pool", "m%d" % q4, wm_bf[:, 4 * q4:4 * q4 + 4, :], wm_v[:, 4 * q4:4 * q4 + 4, :], writes=[("wm", q4)])
        dma("pool", "m4", wq_bf, I["wq"].rearrange("(k p) n -> p k n", p=128), writes=["wq"])
        dma("pool", "m5", wk_bf, I["wk"].rearrange("(k p) n -> p k n", p=128), writes=["wk"])
        act(th16, c_col, AF.Tanh, scale=0.5, reads=["c_col"], writes=["th16"])
        ts("dve", th16, th16, 0.5, ALU.mult, 0.5, ALU.add, reads=["th16"], writes=["th16"])
        tt("dve", sc_f, th16, c_col, ALU.mult, reads=["th16", "c_col"], writes=["sc_f"])
        for q4 in range(4):
            wb = wst[q4 % 2]
            for kk in range(4):
                k = 4 * q4 + kk
                for nchk in range(2):
                    mm(ps[nchk][0:1, :], sc_f[:, k:k + 1], wb[:, kk, nchk * 512:(nchk + 1) * 512],
                       k == 0, k == 15, nchk, reads=["sc_f", ("wst", q4 % 2)])
            if q4 + 2 < 4:
                dma("sp", "w%d" % (q4 % 2), wst[q4 % 2][:, :, 0:1024], wada_v[:, 4 * (q4 + 2):4 * (q4 + 2) + 4, 0:1024],
                    writes=[("wst", q4 % 2)])
        for nchk in range(2):
            tt("dve", mod_sb[0:1, nchk * 512:(nchk + 1) * 512], ps[nchk][0:1, :], bada[0:1, nchk * 512:(nchk + 1) * 512],
               ALU.add, reads=["bada"], writes=[("mod", nchk)], banks=[nchk])
        dma("sp", "c3", agin.ap(), mod_sb[0:1, 0:1024], reads=[("mod", 0), ("mod", 1)], writes=["agin"])
        S.collective(lambda e: e.collective_compute("AllGather", ALU.bypass, replica_groups=GROUPS,
                                                    ins=[agin.ap().opt()], outs=[agout.ap().opt()]),
                     reads=["agin"], writes=["agout"])
        for q4 in range(4):
            dma("sp", "w%d" % (q4 % 2), wst[q4 % 2][:, :, 0:512], wada_v[:, 4 * q4:4 * q4 + 4, 1024:1536], writes=[("wst", q4 % 2)])
            for kk in range(4):
                k = 4 * q4 + kk
                mm(ps[2][0:1, :], sc_f[:, k:k + 1], wst[q4 % 2][:, kk, 0:512], k == 0, k == 15, 2, reads=["sc_f", ("wst", q4 % 2)])
        tt("dve", mod_sb[0:1, 1024:1536], ps[2][0:1, :], bada[0:1, 1024:1536], ALU.add, reads=["bada"], writes=[("mod", 2)], banks=[2])
        agflat = agout.ap().rearrange("a (o n) -> o (a n)", o=1)
        dma("sp", "c0", scale_bc, agflat[:, 2048:4096].partition_broadcast(128), reads=["agout"], writes=["scale_bc"])
        dma("sp", "c1", sh16[0:16, :], agout.ap().rearrange("a (k n) -> (a k) n", n=128)[0:16, :],
            reads=["agout"], writes=["sh16"])
        dma("sp", "c4", gate_d.ap(), mod_sb[0:1, 1024:1536], reads=[("mod", 2)], writes=["gate_d"])
        dma("sp", "c4", gate_bc, gate_d.ap().partition_broadcast(128), reads=["gate_d"], writes=["gate_bc"])
        if debug:
            dma("sp", "c5", dbg["mod"], agout.ap(), reads=["agout"], writes=["dbg_mod"])
        stt(gs_bc, scale_bc, 1.0, lng_bc, ALU.add, ALU.mult, reads=["scale_bc", "lng_bc"], writes=["gs_bc"])
        tr(ps[3][:, 0:16], sh16[0:16, :], identf[0:16, 0:16], 3, reads=["sh16", "identf"])
        cp("dve", shift_col, ps[3][:, 0:16], writes=["shift_col"], banks=[3])
        cp("dve", shift_bf, shift_col, reads=["shift_col"], writes=["shift_bf"])
        for k in range(16):
            ts("dve", shiftrep[:, k, :], onesf, shift_col[:, k:k + 1], ALU.mult,
               reads=["onesf", "shift_col"], writes=["shiftrep"])
        if upto == 0:
            return finish(["dbg_mod"])
        S.barrier()
        off[0] = G0

        def front_end(n, a, xs, xg, junk, hT, hkey, sscol, gen=False):
            g = _front_end(n, a, xs, xg, junk, hT, hkey, sscol)
            if gen:
                return g
            for _ in g:
                pass

        def _front_end(n, a, xs, xg, junk, hT, hkey, sscol):
            xb = xs[n % 2]
            gb = xg[n % 2]
            dma("sp", "x%d" % (n % 2), xb, I["x"][n * 128:(n + 1) * 128, :], writes=[("xs", n % 2)])
            act(junk, xb, AF.Square, accum=ss[:, sscol:sscol + 1], reads=[("xs", n % 2)], writes=[("ss", sscol)])
            ts("dve", rstd[:, sscol:sscol + 1], ss[:, sscol:sscol + 1], 1.0 / D, ALU.mult, EPS, ALU.add,
               reads=[("ss", sscol)], writes=[("ms", sscol)])
            tt("pool", rstd[:, sscol:sscol + 1], rstd[:, sscol:sscol + 1], neghalf[:, 0:1], ALU.pow,
               reads=[("ms", sscol), "neghalf"], writes=[("rstd", sscol)])
            yield
            stt(gb, xb, rstd[:, sscol:sscol + 1], gs_bc, ALU.mult, ALU.mult,
                reads=[("xs", n % 2), ("rstd", sscol), "gs_bc"], writes=[("xg", n % 2)])
            yield
            yield
            for r in range(2):
                for kk in range(8):
                    k = 8 * r + kk
                    tr(psb[0][:, kk * 128:(kk + 1) * 128], gb[:, k * 128:(k + 1) * 128], identb, 0,
                       reads=[("xg", n % 2), "identb"])
                src = psb[0][:, :].rearrange("p (k t) -> p k t", k=8)
                dst = hT[:, 8 * r:8 * r + 8, a * 128:(a + 1) * 128]
                if r == 0:
                    act(dst, src, AF.Copy, writes=[(hkey, a, r)], banks=[0])
                else:
                    cp("dve", dst, src, writes=[(hkey, a, r)], banks=[0])
                yield

        PM0 = off[0]
        xs = [alloc((D,), F32) for _ in range(2)]
        xg = [alloc((D,), BF16) for _ in range(2)]
        junk = alloc((D,), BF16)
        hT = alloc((16, 512), BF16)
        bm_fm = alloc((2,), F32)
        bmtm_bc = alloc((770,), F32)
        biasTM = alloc((770,), F32)
        biasFM = alloc((2,), F32)
        convw = alloc((2, 4), F32)
        convb = alloc((2,), F32)
        normw_bc = alloc((256,), F32)
        skip_bc = alloc((256,), F32)
        fbias = alloc((1,), F32)
        nfb = alloc((1,), F32)
        mx = alloc((2, 515), F32)
        cacc = alloc((2, 512), F32)
        cth = alloc((2, 512), F32)
        xcT = alloc((2, 512), BF16)
        qT = alloc((2, 512), BF16)
        kT = alloc((2, 512), BF16)
        Cst = alloc((2, 257), F32)
        Cbf = alloc((2, 257), BF16)
        if_sb = alloc((4, 2), F32)
        g_sp = alloc((4,), F32)
        g_t = alloc((4,), F32)
        g_u = alloc((4,), F32)
        g_w = alloc((4,), F32)
        g_dec = alloc((4,), F32)
        PT = []
        for _ in range(4):
            PT.append(dict(
                vaug=alloc((257,), BF16), to=alloc((256,), F32), tho=alloc((256,), F32),
                tz=alloc((256,), F32), thz=alloc((256,), F32), ktm=alloc((256,), BF16),
                xstm=alloc((256,), F32), Ssb=alloc((128,), BF16), hv=alloc((256,), F32),
                sm=alloc((16,), F32), ym=alloc((256,), BF16), ymT=alloc((2, 128), BF16),
                z1=alloc((256,), F32)))
        assert off[0] <= WTOP, ("phase M overlaps preloaded weights", off[0], WTOP)
        assert off[0] <= AW

        dma("sp", "c0", bm_fm, I["bm_fm"], writes=["bm_fm"])
        dma("sp", "c1", bmtm_bc, I["bm_tm"].partition_broadcast(128), writes=["bmtm_bc"])
        dma("sp", "c2", convw, I["convw"].rearrange("p (c j) -> p c j", c=2), writes=["convw"])
        dma("sp", "c3", convb, I["convb"], writes=["convb"])
        dma("sp", "c4", normw_bc, I["normw"].partition_broadcast(128), writes=["normw_bc"])
        dma("sp", "c0", skip_bc, I["skip"].partition_broadcast(128), writes=["skip_bc"])
        dma("sp", "c1", fbias, I["fbias"].partition_broadcast(128), writes=["fbias"])
        ts("dve", nfb, fbias, -1.0, ALU.mult, reads=["fbias"], writes=["nfb"])
        WM_ALL = [("wm", q) for q in range(4)]
        for g, (c0, w) in enumerate(((256, 512), (768, 258))):
            for k in range(16):
                mm(ps[1 + g][:, 0:w], shiftrep[:, k, :], wm_bf[:, k, c0:c0 + w], k == 0, k == 15, 1 + g,
                   reads=["shiftrep", ("wm", k // 4)])
            tt("dve", biasTM[:, c0 - 256:c0 - 256 + w], ps[1 + g][:, 0:w], bmtm_bc[:, c0 - 256:c0 - 256 + w], ALU.add,
               reads=["bmtm_bc"], writes=["biasTM"], banks=[1 + g])
        for c in range(2):
            for k in range(16):
                mm(ps[3][:, c:c + 1], wm_bf[:, k, c * 128:(c + 1) * 128], shift_bf[:, k:k + 1], k == 0, k == 15, 3,
                   reads=["shift_bf", ("wm", k // 4)])
        tt("dve", biasFM, ps[3][:, 0:2], bm_fm, ALU.add, reads=["bm_fm"], writes=["biasFM"], banks=[3])
        memset("dve", mx[:, :, 0:3], 0.0, writes=["mx_carry"])
        memset("dve", Cst, 0.0, writes=[("Cst", 0), ("Cst", 1)])
        memset("pool", Cbf, 0.0, writes=["Cbf"])
        for pb in PT:
            memset("pool", pb["vaug"][:, 256:257], 1.0, writes=[])

        ymT_v = [t_.ap().rearrange("(c p) t -> p c t", p=128) for t_ in ymT_d]
        XC = [("xcT", 0), ("xcT", 1)]
        HT_ALL = [("hT", a, r) for a in range(4) for r in range(2)]

        def run_interleaved(*gens):
            gens = [g for g in gens if g is not None]
            while gens:
                for g in list(gens):
                    try:
                        next(g)
                    except StopIteration:
                        gens.remove(g)

        def FE_gen(s_):
            for a in range(4):
                for _ in front_end(4 * s_ + a, a, xs, xg, junk, hT, "hT", 4 * s_ + a, gen=True):
                    yield

        def FM_m(s_):
            for c in range(2):
                bk = 1 + c
                for k in range(16):
                    mm(ps[bk][:, :], wm_bf[:, k, c * 128:(c + 1) * 128], hT[:, k, :], k == 0, k == 15, bk,
                       reads=HT_ALL + [("wm", k // 4)])
                act(mx[:, c, 3:515], ps[bk][:, :], AF.Identity, bias=biasFM[:, c:c + 1],
                    reads=["biasFM", "mx_carry"], writes=[("mx", c)], banks=[bk])
            for c in range(2):
                ts("dve", cacc[:, c, :], mx[:, c, 0:512], convw[:, c, 0:1], ALU.mult, convb[:, c:c + 1], ALU.add,
                   reads=[("mx", c), "mx_carry", "convw", "convb"], writes=[("cacc", c)])
            for j in range(1, 4):
                for c in range(2):
                    stt(cacc[:, c, :], mx[:, c, j:j + 512], convw[:, c, j:j + 1], cacc[:, c, :], ALU.mult, ALU.add,
                        reads=[("mx", c), "mx_carry", ("cacc", c)], writes=[("cacc", c)])
            for c in range(2):
                act(cth[:, c, :], cacc[:, c, :], AF.Tanh, scale=0.5, reads=[("cacc", c)], writes=[("cth", c)])
            for c in range(2):
                ts("dve", cth[:, c, :], cth[:, c, :], 0.5, ALU.mult, 0.5, ALU.add, reads=[("cth", c)], writes=[("cth", c)])
            for c in range(2):
                tt("dve", xcT[:, c, :], cth[:, c, :], cacc[:, c, :], ALU.mult,
                   reads=[("cth", c), ("cacc", c)], writes=[("xcT", c)])
            cp("dve", mx[:, :, 0:3], mx[:, :, 512:515], reads=[("mx", 0), ("mx", 1)], writes=["mx_carry"])
            for ec in range(2):
                for (wt, dst, key, bk) in ((wq_bf, qT, "qT", 1), (wk_bf, kT, "kT", 2)):
                    for dk in range(2):
                        mm(ps[bk][:, :], wt[:, dk, ec * 128:(ec + 1) * 128], xcT[:, dk, :], dk == 0, dk == 1, bk,
                           reads=XC + ["wq", "wk"])
                    if key == "qT":
                        act(dst[:, ec, :], ps[bk][:, :], AF.Copy, writes=[(key, ec)], banks=[bk])
                    else:
                        ts("dve", dst[:, ec, :], ps[bk][:, :], 1.0 / 16.0, ALU.mult, writes=[(key, ec)], banks=[bk])

        def gates_st(s_):
            for a in range(4):
                for k in range(16):
                    mm(ps[3][:, 8 + 2 * a:10 + 2 * a], hT[:, k, a * 128:(a + 1) * 128], wm_bf[:, k, 1024:1026], k == 0, k == 15, 3,
                       reads=[("hT", a, 0), ("hT", a, 1), ("wm", 3)])
            for a in range(4):
                tt("dve", if_sb[:, a, :], ps[3][:, 8 + 2 * a:10 + 2 * a], biasTM[:, 768:770], ALU.add,
                   reads=["biasTM"], writes=["if"], banks=[3])
            act(g_sp, if_sb[:, :, 1], AF.Exp, bias=nfb, scale=-1.0, reads=["if", "nfb"], writes=["gsp"])
            act(g_sp, g_sp, AF.Ln, bias=1.0, reads=["gsp"], writes=["gsp"])
            mm(ps[3][:, 0:4], triu, g_sp, True, True, 3, reads=["triu", "gsp"])
            mm(ps[3][:, 4:8], onesf, g_sp, True, True, 3, reads=["onesf", "gsp"])
            tt("dve", g_t, ps[3][:, 0:4], if_sb[:, :, 0], ALU.add, reads=["if"], writes=["gt"], banks=[3])
            act(g_w, ps[3][:, 0:4], AF.Exp, scale=-1.0, writes=["gw"], banks=[3])
            act(g_dec, ps[3][:, 4:8], AF.Exp, scale=-1.0, writes=["gdec"], banks=[3])
            act(g_u, g_t, AF.Exp, reads=["gt"], writes=["gu"])

        def A1_gen(s_, a):
            n = 4 * s_ + a
            pb = PT[n % 4]
            pk = ("pt", n % 4)

            def evac0():
                tt("dve", pb["vaug"][:, 0:256], ps[1][:, 0:256], biasTM[:, 0:256], ALU.add,
                   reads=["biasTM"], writes=[(pk, "vaug")], banks=[1])
                tt("dve", pb["to"], ps[1][:, 256:512], biasTM[:, 256:512], ALU.add,
                   reads=["biasTM"], writes=[(pk, "to")], banks=[1])

            for g, (c0, w) in enumerate(((256, 512), (768, 256))):
                bk = 1 + g
                for k in range(16):
                    mm(ps[bk][:, 0:w], hT[:, k, a * 128:(a + 1) * 128], wm_bf[:, k, c0:c0 + w], k == 0, k == 15, bk,
                       reads=[("hT", a, 0), ("hT", a, 1), ("wm", k // 4)])
                    if k % 4 == 3:
                        yield
                    if g == 1 and k == 7:
                        evac0()
            yield
            act(pb["tho"], pb["to"], AF.Tanh, scale=0.5, reads=[(pk, "to")], writes=[(pk, "tho")])
            yield
            tt("dve", pb["tz"], ps[2][:, 0:256], biasTM[:, 512:768], ALU.add,
               reads=["biasTM"], writes=[(pk, "tz")], banks=[2])
            yield
            act(pb["thz"], pb["tz"], AF.Tanh, scale=0.5, reads=[(pk, "tz")], writes=[(pk, "thz")])
            yield
            yield
            stt(pb["z1"], pb["thz"], 1.0, pb["tz"], ALU.add, ALU.mult,
                reads=[(pk, "thz"), (pk, "tz")], writes=[(pk, "z1")])

        def A2_gen(s_, a):
            n = 4 * s_ + a
            pb = PT[n % 4]
            pk = ("pt", n % 4)
            for dk in range(2):
                mm(ps[7][:, 0:256], xcT[:, dk, a * 128:(a + 1) * 128], wk_bf[:, dk, :], dk == 0, dk == 1, 7,
                   reads=XC + ["wk"])
            for dk in range(2):
                tr(psb[7][:, 512 + dk * 128:512 + (dk + 1) * 128], xcT[:, dk, a * 128:(a + 1) * 128], identb, 7,
                   reads=XC + ["identb"])
            for ec in range(2):
                mm(ps[3][:, 128:256], kT[:, ec, a * 128:(a + 1) * 128], qT[:, ec, a * 128:(a + 1) * 128],
                   ec == 0, ec == 1, 3, reads=[("kT", ec), ("qT", ec)])
            yield
            yield
            yield
            tt("dve", pb["xstm"], psb[7][:, 512:768], skip_bc, ALU.mult, reads=["skip_bc"], writes=[(pk, "xstm")], banks=[7])
            yield
            ts("dve", pb["ktm"], ps[7][:, 0:256], g_u[:, a:a + 1], ALU.mult, 1.0 / 16.0, ALU.mult,
               reads=["gu"], writes=[(pk, "ktm")], banks=[7])
            yield
            stt(pb["Ssb"], ps[3][:, 128:256], g_u[:, a:a + 1], triu, ALU.mult, ALU.mult,
                reads=["gu", "triu"], writes=[(pk, "Ssb")], banks=[3])
            yield

        def B_gen(s_, a):
            n = 4 * s_ + a
            pb = PT[n % 4]
            pk = ("pt", n % 4)
            sm = pb["sm"]
            mm(ps[4][:, 0:257], pb["Ssb"], pb["vaug"], True, False, 4, reads=[(pk, "Ssb"), (pk, "vaug")])
            for dc in range(2):
                mm(ps[4][:, 0:257], qT[:, dc, a * 128:(a + 1) * 128], Cbf[:, dc, :], False, dc == 1, 4,
                   reads=[("qT", dc), "Cbf"])
            for dc in range(2):
                mm(ps[5 + dc][:, 0:257], pb["ktm"][:, dc * 128:(dc + 1) * 128], pb["vaug"], True, True, 5 + dc,
                   reads=[(pk, "ktm"), (pk, "vaug")])
            yield
            for dc in range(2):
                tt("dve", Cst[:, dc, :], ps[5 + dc][:, 0:257], Cst[:, dc, :], ALU.add,
                   reads=[("Cst", dc)], writes=[("Cst", dc)], banks=[5 + dc])
            tt("dve", sm[:, 0:1], ps[4][:, 256:257], g_w[:, a:a + 1], ALU.mult, reads=["gw"], writes=[(pk, "sm0")], banks=[4])
            yield
            for dc in range(2):
                act(Cbf[:, dc, :], Cst[:, dc, :], AF.Copy, scale=g_dec[:, a:a + 1], reads=[("Cst", dc), "gdec"], writes=["Cbf"])
            ts("dve", sm[:, 14:15], sm[:, 0:1], -1.0, ALU.mult, reads=[(pk, "sm0")], writes=[(pk, "sm14")])
            yield
            for dc in range(2):
                ts("dve", Cst[:, dc, :], Cst[:, dc, :], g_dec[:, a:a + 1], ALU.mult, reads=[("Cst", dc), "gdec"], writes=[("Cst", dc)])
            tt("dve", sm[:, 1:2], sm[:, 0:1], sm[:, 14:15], ALU.max, reads=[(pk, "sm0"), (pk, "sm14")], writes=[(pk, "sm1")])
            yield
            ts("dve", sm[:, 1:2], sm[:, 1:2], 1.0, ALU.max, reads=[(pk, "sm1")], writes=[(pk, "sm1")])
            yield
            recip(sm[:, 2:3], sm[:, 1:2], reads=[(pk, "sm1")], writes=[(pk, "sm2")])
            yield
            tt("dve", sm[:, 3:4], sm[:, 2:3], g_w[:, a:a + 1], ALU.mult, reads=[(pk, "sm2"), "gw"], writes=[(pk, "sm3")])
            yield
            act(pb["hv"], ps[4][:, 0:256], AF.Copy, scale=sm[:, 3:4], reads=[(pk, "sm3")], writes=[(pk, "hv")], banks=[4])
            yield

        def B2_gen(s_, a):
            n = 4 * s_ + a
            pb = PT[n % 4]
            pk = ("pt", n % 4)
            sm = pb["sm"]
            S.op("dve", lambda e, pb=pb, sm=sm: e.bn_stats(out=sm[:, 4:10], in_=pb["hv"]), reads=[(pk, "hv")], writes=[(pk, "bn6")])
            yield
            S.op("dve", lambda e, sm=sm: e.bn_aggr(out=sm[:, 10:12], in_=sm[:, 4:10]), reads=[(pk, "bn6")], writes=[(pk, "mv")])
            yield
            ts("dve", sm[:, 12:13], sm[:, 11:12], EPS, ALU.add, reads=[(pk, "mv")], writes=[(pk, "ve")])
            yield
            tt("pool", sm[:, 13:14], sm[:, 12:13], neghalf[:, 0:1], ALU.pow, reads=[(pk, "ve"), "neghalf"], writes=[(pk, "rs2")])
            yield
            ts("dve", pb["hv"], pb["hv"], sm[:, 10:11], ALU.subtract, sm[:, 13:14], ALU.mult,
               reads=[(pk, "hv"), (pk, "mv"), (pk, "rs2")], writes=[(pk, "hv")])
            yield
            tt("dve", pb["hv"], pb["hv"], normw_bc, ALU.mult, reads=[(pk, "hv"), "normw_bc"], writes=[(pk, "hv")])
            yield
            stt(pb["hv"], pb["tho"], 1.0, pb["hv"], ALU.add, ALU.mult, reads=[(pk, "tho"), (pk, "hv")], writes=[(pk, "hv")])
            yield
            stt(pb["hv"], pb["hv"], 0.5, pb["xstm"], ALU.mult, ALU.add, reads=[(pk, "hv"), (pk, "xstm")], writes=[(pk, "hv")])
            yield
            stt(pb["ym"], pb["hv"], 0.5, pb["z1"], ALU.mult, ALU.mult, reads=[(pk, "hv"), (pk, "z1")], writes=[(pk, "ym")])
            yield
            for dc in range(2):
                tr(psb[7][:, dc * 128:(dc + 1) * 128], pb["ym"][:, dc * 128:(dc + 1) * 128], identb, 7,
                   reads=[(pk, "ym"), "identb"])
            yield
            cp("dve", pb["ymT"], psb[7][:, 0:256].rearrange("p (c t) -> p c t", c=2), writes=[(pk, "ymT")], banks=[7])
            dma("pool", "y%d" % (n % 2), ymT_v[n // 16][:, :, (n % 16) * 128:(n % 16 + 1) * 128], pb["ymT"], reads=[(pk, "ymT")], writes=[("ymT_d", n // 16)])

        def prefetch_nsa_weights():
            wn_v = I["wn"].rearrange("(k p) n -> p k n", p=128)
            for q4 in range(4):
                dma("pool", "v%d" % q4, wn_bf[:, 4 * q4:4 * q4 + 4, :], wn_v[:, 4 * q4:4 * q4 + 4, :], writes=[("wn", q4)])
            dma("pool", "v4", w1[0], I["w1k"].rearrange("(m p) h -> p m h", p=128), writes=[("w1", 0)])
            dma("pool", "v5", w1[1], I["w1v"].rearrange("(m p) h -> p m h", p=128), writes=[("w1", 1)])
            dma("pool", "v6", w2k_bf, I["w2k"], writes=[("w2", 0)])
            dma("pool", "v8", w2v_bf, I["w2v"], writes=[("w2", 1)])
            dma("pool", "v7", pos2[0], I["posk2"], writes=[("pos2", 0)])
            dma("pool", "v9", pos2[1], I["posv2"], writes=[("pos2", 1)])

        if mtiles:
            run_interleaved(FE_gen(0))
        if not mtiles:
            prefetch_nsa_weights()
        for s_ in range(mtiles):
            FM_m(s_)
            gates_st(s_)
            if s_ == 0:
                prefetch_nsa_weights()
            run_interleaved(A1_gen(s_, 0))
            run_interleaved(A1_gen(s_, 1), A2_gen(s_, 0))
            run_interleaved(A1_gen(s_, 2), A2_gen(s_, 1), B_gen(s_, 0))
            for a in range(4):
                g1 = A1_gen(s_, a + 3) if a + 3 < 4 else (FE_gen(s_ + 1) if (a == 1 and s_ + 1 < mtiles) else None)
                g2 = A2_gen(s_, a + 2) if a + 2 < 4 else None
                g3 = B_gen(s_, a + 1) if a + 1 < 4 else None
                run_interleaved(g1, g2, g3, B2_gen(s_, a))
        if debug:
            for hf in range(2):
                if mtiles > 4 * hf:
                    w_ = min(2048, 512 * mtiles - 2048 * hf)
                    dma("sp", "c5", dbg["ym"][:, 2048 * hf:2048 * hf + w_], ymT_d[hf].ap()[:, 0:w_], reads=[("ymT_d", hf)], writes=["dbg_ym"])
        if upto == 1:
            return finish(["dbg_ym", "dbg_mod"])
        S.barrier()
        off[0] = PM0

        xs = [alloc((D,), F32) for _ in range(2)]
        xg = [alloc((D,), BF16) for _ in range(2)]
        junk = alloc((D,), BF16)
        hTn = [alloc((16, 512), BF16) for _ in range(2)]
        bn_fm = alloc((8,), F32)
        bnfm_raw = alloc((8,), F32)
        bq8 = alloc((4,), F32)
        bntm_bc = alloc((396,), F32)
        biasTMn = alloc((396,), F32)
        biasFMn = alloc((8,), F32)
        Rb = [alloc((4, 4, 128), BF16) for _ in range(2)]
        KE = alloc((T,), BF16)
        KWE = alloc((T,), BF16)
        c2T = [alloc((T + 2,), BF16) for _ in range(2)]
        vsaug = alloc((NT, 65), BF16)
        vwaug = alloc((NT, 65), BF16)
        kcmpE = alloc((256,), BF16)
        gkv = [alloc((256,), BF16) for _ in range(2)]
        vaug = alloc((2, 128), BF16)
        hb16 = alloc((16,), F32)
        rowt = alloc((4, 128), F32)
        abt = alloc((128,), F32)
        cbt = alloc((256,), F32)
        dgt = alloc((4, 128), F32)
        fet = alloc((4, 128), F32)
        Zn = [alloc((4, 256), F32) for _ in range(2)]
        gtn = [alloc((4, 12), F32) for _ in range(2)]
        tzn = [alloc((256,), F32) for _ in range(2)]
        thzn = [alloc((256,), F32) for _ in range(2)]
        tg = [alloc((12,), F32) for _ in range(2)]
        Eb = [alloc((4, 128), BF16) for _ in range(4)]
        tmpE = [alloc((4, 128), F32) for _ in range(2)]
        gch = alloc((4, 64), F32)
        hb64 = alloc((64,), F32)
        imp = alloc((64,), F32)
        rep = alloc((64,), F32)
        m8 = alloc((16,), F32)
        selw = alloc((128,), BF16)
        rz = [alloc((3, 4), F32) for _ in range(2)]
        coef = [alloc((3, 4), F32) for _ in range(2)]
        yacc = [alloc((4, 64), F32) for _ in range(2)]
        yn = [alloc((256,), BF16) for _ in range(2)]
        ynT = [alloc((2, 128), BF16) for _ in range(2)]
        assert off[0] <= AW
        build.arena_n = off[0]

        dma("sp", "c0", bnfm_raw, I["bn_fm"], writes=["bnfm_raw"])
        dma("sp", "c1", bntm_bc, I["bn_tm"].partition_broadcast(128), writes=["bntm_bc"])
        dma("sp", "c2", KE[64:128, :], I["expand"], writes=["KEc"])
        dma("sp", "c3", KWE[64:128, :], I["e0"], writes=["KWEc"])
        dma("sp", "c4", kcmpE[64:128, :], I["e0"][:, 0:256], writes=["kcmpEc"])
        dma("sp", "c0", rowt[64:128, :, :], I["rowt"].rearrange("p (h q) -> p h q", h=4), writes=["rowt"])
        dma("sp", "c1", abt, I["abt"], writes=["abt"])
        dma("sp", "c2", cbt, I["cbt"], writes=["cbt"])
        dma("sp", "c0", dgt, I["dgt"].rearrange("p (h q) -> p h q", h=4), writes=["dgt"])
        dma("sp", "c1", fet, I["fet"].rearrange("p (h q) -> p h q", h=4), writes=["fet"])
        dma("sp", "c3", vaug[:, 0, 64:128], I["ovl1"][:, 0:64], writes=["vaugc"])
        dma("sp", "c4", vaug[:, 1, 64:128], I["ovl1"][:, 64:128], writes=["vaugc"])
        WN_ALL = [("wn", q) for q in range(4)]
        for k in range(16):
            mm(ps[1][:, 0:396], shiftrep[:, k, :], wn_bf[:, k, 640:1036], k == 0, k == 15, 1, reads=["shiftrep", ("wn", k // 4)])
        tt("dve", biasTMn, ps[1][:, 0:396], bntm_bc, ALU.add, reads=["bntm_bc"], writes=["biasTMn"], banks=[1])
        FMC = [(0, 64), (64, 64), (128, 64), (192, 64), (256, 128), (384, 128), (512, 64), (576, 64)]
        memset("dve", biasFMn, 0.0, writes=["biasFMn"])
        for ci, (c0, M) in enumerate(FMC):
            for k in range(16):
                mm(ps[3][0:M, ci:ci + 1], wn_bf[:, k, c0:c0 + M], shift_bf[:, k:k + 1], k == 0, k == 15, 3,
                   reads=["shift_bf", ("wn", k // 4)])
            tt("dve", biasFMn[0:M, ci:ci + 1], ps[3][0:M, ci:ci + 1], bnfm_raw[0:M, ci:ci + 1], ALU.add,
               reads=["bnfm_raw", "biasFMn"], writes=["biasFMn"], banks=[3])
        ts("dve", bq8, biasFMn[:, 0:4], 0.125, ALU.mult, reads=["biasFMn"], writes=["bq8"])
        memset("dve", hb64, 0.0, writes=["hb64"])
        for kv in range(2):
            for m in range(16):
                mm(ps[2][:, kv:kv + 1], w1[kv][:, m, :], pos2[kv][:, m:m + 1], m == 0, m == 15, 2, reads=[("w1", 0), ("w1", 1), ("pos2", 0), ("pos2", 1)])
            ts("dve", hb64[:, kv * 32:(kv + 1) * 32], hb64[:, kv * 32:(kv + 1) * 32], ps[2][:, kv:kv + 1], ALU.add,
               reads=["hb64"], writes=["hb64"], banks=[2])
        for b2 in range(2):
            for a in range(4):
                cp("dve", Rb[b2][64:128, a, :, :], rowt[64:128, :, :], reads=["rowt"], writes=[("Rm", b2, a)])
        memset("dve", selw[:, 0:64], 0.0, writes=["selw0"])
        memset("dve", vsaug[:, :, 64:65], 1.0, writes=["vs1"])
        memset("dve", vwaug[:, :, 64:65], 1.0, writes=["vw1"])
        memset("pool", gkv[0], 0.0, writes=["gk"])
        memset("pool", gkv[1], 0.0, writes=["gv"])
        memset("pool", c2T[0][:, 0:2], 0.0, writes=[])
        memset("pool", c2T[1][:, 0:2], 0.0, writes=[])
        memset("pool", imp[:, 63:64], 0.0, writes=[])

        ynT_v = [t_.ap().rearrange("(c p) t -> p c t", p=128) for t_ in ynT_d]
        scc = [0]
        ebc = [0]

        tmc = [0]
        bgc = [0]
        SCB = (1, 2, 3, 4)
        INPROJ_STEPS = 60
        pending_tail = []
        cbt_v = cbt.rearrange("p (h j n) -> p h j n", h=4, j=2)
        abt_v = abt.rearrange("p (h r) -> p h r", h=4)

        def compress_st(s_):
            j0 = max(0, 32 * s_ - 1)
            j1 = 32 * s_ + 30
            nb = j1 - j0 + 1
            c2keys = [("c2T", kv, s2) for kv in range(2) for s2 in (s_ - 1, s_) if s2 >= 0]
            for kv in range(2):
                for m in range(16):
                    col = 16 * j0 + 2 * m + 1
                    mm(ps[0][:, kv * 32:kv * 32 + nb], w1[kv][:, m, :], c2T[kv][:, col:col + 16 * (nb - 1) + 1:16],
                       m == 0, m == 15, 0, reads=[("w1", 0), ("w1", 1)] + c2keys)
            g0_, g1_, g2_, g3_ = gch[:, 0, :], gch[:, 1, :], gch[:, 2, :], gch[:, 3, :]
            if nb < 32:
                memset("dve", g0_, 0.0, writes=["g0"])
                for kv in range(2):
                    tt("dve", g0_[:, kv * 32:kv * 32 + nb], ps[0][:, kv * 32:kv * 32 + nb], hb64[:, kv * 32:kv * 32 + nb], ALU.add,
                       reads=["hb64", "g0"], writes=["g0"], banks=[0])
            else:
                tt("dve", g0_, ps[0][:, 0:64], hb64, ALU.add, reads=["hb64"], writes=["g0"], banks=[0])
            tt("dve", g1_, g0_, g0_, ALU.mult, reads=["g0"], writes=["g1"])
            ts("dve", g1_, g1_, 0.044715, ALU.mult, 1.0, ALU.add, reads=["g1"], writes=["g1"])
            tt("dve", g1_, g1_, g0_, ALU.mult, reads=["g1", "g0"], writes=["g1"])
            act(g2_, g1_, AF.Tanh, scale=GC, reads=["g1"], writes=["g2"])
            stt(g3_, g2_, 1.0, g0_, ALU.add, ALU.mult, reads=["g2", "g0"], writes=["g3"])
            ts("dve", gkv[0][:, j0:j1 + 1], g3_[:, 0:nb], 0.5, ALU.mult, reads=["g3"], writes=["gk"])
            ts("dve", gkv[1][:, j0:j1 + 1], g3_[:, 32:32 + nb], 0.5, ALU.mult, reads=["g3"], writes=["gv"])

        def compress_st2(s_):
            j0 = max(0, 32 * s_ - 1)
            j1 = 32 * s_ + 30
            nb = j1 - j0 + 1
            mm(ps[0][0:64, 64:64 + nb], w2k_bf, gkv[0][:, j0:j1 + 1], True, True, 0, reads=[("w2", 0), ("w2", 1), "gk"])
            cp("dve", kcmpE[0:64, j0:j1 + 1], ps[0][0:64, 64:64 + nb], writes=["kcmpE"], banks=[0])
            for jc in range(j0 // 128, j1 // 128 + 1):
                mm(ps[0][:, 128 + 64 * jc:192 + 64 * jc], gkv[1][:, jc * 128:(jc + 1) * 128], w2v_bf, True, True, 0,
                   reads=[("w2", 0), ("w2", 1), "gv"])
                cp("dve", vaug[:, jc, 0:64], ps[0][:, 128 + 64 * jc:192 + 64 * jc], writes=[("vaug", jc)], banks=[0])

        def nsa_supertile(s_, R, bg=None, pace=1):
            LA = 4
            oc = ps[5][:, :].rearrange("p (h c) -> p h c", h=4)
            ow = ps[6][:, 0:260].rearrange("p (h c) -> p h c", h=4)
            osl = ps[7][:, 0:260].rearrange("p (h c) -> p h c", h=4)
            acc = {"cmp": (oc, 5), "win": (ow, 6), "sel": (osl, 7)}
            Rq = [("Rq", s_ % 2, h) for h in range(4)]
            jobs = []
            last_of = {}
            first_sel = {}
            for a in range(4):
                n = 4 * s_ + a
                ntot = 8 * n + 7
                for jc in range(2):
                    if ntot > 128 * jc:
                        jobs.append((n, a, "cmp", jc, min(128, ntot - 128 * jc)))
                        last_of[(n, "cmp")] = len(jobs) - 1
                for kc in range(max(0, n - 4), n + 1):
                    jobs.append((n, a, "win", kc, 128))
                    last_of[(n, "win")] = len(jobs) - 1
                first_sel[n] = len(jobs)
                for kc in range(0, n + 1):
                    jobs.append((n, a, "sel", kc, 128))
                    last_of[(n, "sel")] = len(jobs) - 1
            NJ = len(jobs)
            first = {}
            sbk = {}
            Ek = {}
            tkq = {}
            cmp_done = set()

            def qk(i):
                n, a, kind, c, nj = jobs[i]
                c0 = a * 128
                sb = SCB[scc[0] % 4]
                scc[0] += 1
                sbk[i] = sb
                if kind == "cmp":
                    lhs = kcmpE[:, c * 128:c * 128 + nj]
                    rd = ["kcmpE", "kcmpEc"]
                elif kind == "win":
                    lhs = KWE[:, c * 128:(c + 1) * 128]
                    rd = [("KWE", c // 4), "KWEc"]
                else:
                    lhs = KE[:, c * 128:(c + 1) * 128]
                    rd = [("KE", c // 4), "KEc"]
                mm(ps[sb][0:nj, 0:512], lhs, R[:, a, :, :].rearrange("p h q -> p (h q)"), True, True, sb, reads=rd + Rq + [("Rm", s_ % 2, a)])

            def ex(i):
                n, a, kind, c, nj = jobs[i]
                sb = sbk[i]
                E = Eb[ebc[0] % 4]
                ek = ("E", ebc[0] % 4)
                ebc[0] += 1
                Ek[i] = (E, ek)
                tm = tmpE[tmc[0] % 2]
                tk = ("tmpE", tmc[0] % 2)
                tmc[0] += 1
                psv = ps[sb][0:nj, 0:512].rearrange("p (h q) -> p h q", h=4)
                if kind == "cmp":
                    boundary = 128 * c + nj - 1 >= 8 * n - 1
                    bv = cbt_v[0:nj, :, c, n:n + 1]
                    bkey = "cbt"
                else:
                    boundary = c == n
                    bv = abt_v[0:nj, :, n - c:n - c + 1]
                    bkey = "abt"
                if kind != "cmp" and c == n:
                    tt("dve", tm, psv, dgt, ALU.add, reads=["dgt"], writes=[tk], banks=[sb])
                elif kind == "win" and c == n - 4:
                    tt("dve", tm, psv, fet, ALU.add, reads=["fet"], writes=[tk], banks=[sb])
                else:
                    tt("dve", tm[0:nj, :, :], psv, bv.to_broadcast([nj, 4, 128]), ALU.add, reads=[bkey], writes=[tk], banks=[sb])
                act(E[0:nj, :, :], tm[0:nj, :, :], AF.Exp, reads=[tk], writes=[ek])
                if kind == "cmp" and boundary:
                    asel(E[0:nj, :, :], E[0:nj, :, :], [[0, 4], [1, 128]], 128 * n - 31 - 2048 * c, -16, reads=[ek], writes=[ek])

            def pv(i):
                n, a, kind, c, nj = jobs[i]
                E, ek = Ek[i]
                ob, obank = acc[kind]
                for h in range(4):
                    st_ = not first.get((n, kind), False)
                    first[(n, kind)] = True
                    if kind == "cmp":
                        mm(ob[:, h, :], E[0:nj, h, :], vaug[0:nj, c, :], st_, False, obank, reads=[ek, ("vaug", c), "vaugc"])
                    elif kind == "win":
                        mm(ob[:, h, :], E[:, h, :], vwaug[:, c, :], st_, False, obank, reads=[ek, ("vw", c), "vw1"])
                    else:
                        mm(ob[:, h, :], E[:, h, :], vsaug[:, c, :], st_, False, obank, reads=[ek, ("vs", c), "vs1"])

            def topk_groups(n, a):
                c0 = a * 128
                p2 = n % 2
                rmk = ("Rm", s_ % 2, a)
                gv3 = gtn[s_ % 2][:, a, :].rearrange("p (h b) -> p h b", b=3)

                def tk0():
                    ts("dve", rz[p2][:, 0, :], oc[:, :, 64], 1e-37, ALU.max, writes=[("rz", p2, 0)], banks=[5])
                    ts("dve", imp[:, 0:63], oc[:, 0, 65:128], 1.0, ALU.mult, writes=["imp"], banks=[5])
                    recip(rz[p2][:, 0, :], rz[p2][:, 0, :], reads=[("rz", p2, 0)], writes=[("rz", p2, 0)])

                def tk1():
                    ts("dve", imp[:, 0:63], imp[:, 0:63], rz[p2][:, 0, 0:1], ALU.mult, reads=[("rz", p2, 0), "imp"], writes=["imp"])
                    for h in range(1, 4):
                        stt(imp[:, 0:63], oc[:, h, 65:128], rz[p2][:, 0, h:h + 1], imp[:, 0:63], ALU.mult, ALU.add,
                            reads=[("rz", p2, 0), "imp"], writes=["imp"], banks=[5])
                    stt(coef[p2][:, 0, :], gv3[:, :, 0], 1.0, rz[p2][:, 0, :], ALU.add, ALU.mult,
                        reads=[("gtn", s_ % 2, a), ("rz", p2, 0)], writes=[("coef", p2, 0)])

                def tk2():
                    if 2 * n + 1 < 64:
                        memset("dve", imp[0:64, 2 * n + 1:64], -BIG, writes=["imp"])
                    if 2 * n + 2 < 64:
                        memset("dve", imp[64:128, 2 * n + 2:64], -BIG, writes=["imp"])
                    memset("dve", imp[0:64, 2 * n:2 * n + 1], BIG, writes=["imp"])
                    memset("dve", imp[64:128, 2 * n + 1:2 * n + 2], BIG, writes=["imp"])
                    memset("dve", imp[:, 0:1], BIG, writes=["imp"])
                    S.op("dve", lambda e: e.max(out=m8[:, 0:8], in_=imp), reads=["imp"], writes=["m8a"])
                    for h in range(4):
                        ts("dve", yacc[p2][:, h, :], oc[:, h, 0:64], coef[p2][:, 0, h:h + 1], ALU.mult,
                           reads=[("coef", p2, 0)], writes=[("yacc", p2, h)], banks=[5])

                def tk3():
                    S.op("dve", lambda e: e.match_replace(out=rep, in_to_replace=m8[:, 0:8], in_values=imp, imm_value=-BIG),
                         reads=["imp", "m8a"], writes=["rep"])
                    S.op("dve", lambda e: e.max(out=m8[:, 8:16], in_=rep), reads=["rep"], writes=["m8b"])

                def tk4():
                    ts("dve", selw[:, 64:128], imp, m8[:, 15:16], ALU.is_ge, 1.0, ALU.subtract, reads=["imp", "m8b"], writes=["selw"])
                    tr(psb[0][:, 0:128], selw, identb, 0, reads=["selw", "selw0", "identb"])
                    for h in range(4):
                        stt(R[64:128, a, h, :], psb[0][64:128, 0:128], NEGM, rowt[64:128, h, :], ALU.mult, ALU.add,
                            reads=["rowt"], writes=[rmk], banks=[0])
                if n < 8:
                    def tk0s():
                        ts("dve", rz[p2][:, 0, :], oc[:, :, 64], 1e-37, ALU.max, writes=[("rz", p2, 0)], banks=[5])
                        recip(rz[p2][:, 0, :], rz[p2][:, 0, :], reads=[("rz", p2, 0)], writes=[("rz", p2, 0)])

                    def tk1s():
                        stt(coef[p2][:, 0, :], gv3[:, :, 0], 1.0, rz[p2][:, 0, :], ALU.add, ALU.mult,
                            reads=[("gtn", s_ % 2, a), ("rz", p2, 0)], writes=[("coef", p2, 0)])

                    def tk2s():
                        for h in range(4):
                            ts("dve", yacc[p2][:, h, :], oc[:, h, 0:64], coef[p2][:, 0, h:h + 1], ALU.mult,
                               reads=[("coef", p2, 0)], writes=[("yacc", p2, h)], banks=[5])
                    return [tk0s, tk1s, tk2s]
                return [tk0, tk1, tk2, tk3, tk4]

            def win_done(n, a):
                p2 = n % 2
                gv3 = gtn[s_ % 2][:, a, :].rearrange("p (h b) -> p h b", b=3)
                ts("dve", rz[p2][:, 2, :], ow[:, :, 64], 1e-37, ALU.max, writes=[("rz", p2, 2)], banks=[6])
                recip(rz[p2][:, 2, :], rz[p2][:, 2, :], reads=[("rz", p2, 2)], writes=[("rz", p2, 2)])
                stt(coef[p2][:, 2, :], gv3[:, :, 2], 1.0, rz[p2][:, 2, :], ALU.add, ALU.mult,
                    reads=[("gtn", s_ % 2, a), ("rz", p2, 2)], writes=[("coef", p2, 2)])
                for h in range(4):
                    stt(yacc[p2][:, h, :], ow[:, h, 0:64], coef[p2][:, 2, h:h + 1], yacc[p2][:, h, :], ALU.mult, ALU.add,
                        reads=[("coef", p2, 2), ("yacc", p2, h)], writes=[("yacc", p2, h)], banks=[6])

            def sel_done(n, a):
                p2 = n % 2
                gv3 = gtn[s_ % 2][:, a, :].rearrange("p (h b) -> p h b", b=3)
                ts("dve", rz[p2][:, 1, :], osl[:, :, 64], 1e-37, ALU.max, writes=[("rz", p2, 1)], banks=[7])
                recip(rz[p2][:, 1, :], rz[p2][:, 1, :], reads=[("rz", p2, 1)], writes=[("rz", p2, 1)])
                stt(coef[p2][:, 1, :], gv3[:, :, 1], 1.0, rz[p2][:, 1, :], ALU.add, ALU.mult,
                    reads=[("gtn", s_ % 2, a), ("rz", p2, 1)], writes=[("coef", p2, 1)])
                for h in range(4):
                    stt(yacc[p2][:, h, :], osl[:, h, 0:64], coef[p2][:, 1, h:h + 1], yacc[p2][:, h, :], ALU.mult, ALU.add,
                        reads=[("coef", p2, 1), ("yacc", p2, h)], writes=[("yacc", p2, h)], banks=[7])
                yb = yn[n % 2]
                stt(yb, yacc[p2][:, :, :].rearrange("p h c -> p (h c)"), 0.25, Zn[s_ % 2][:, a, :], ALU.mult, ALU.mult,
                    reads=[("yacc", p2, h) for h in range(4)] + [("Zn", s_ % 2, a)], writes=[("yn", n % 2)])

                def tail(n=n, yb=yb):
                    for dc in range(2):
                        tr(psb[0][:, 256 + dc * 128:256 + (dc + 1) * 128], yb[:, dc * 128:(dc + 1) * 128], identb, 0,
                           reads=[("yn", n % 2), "identb"])
                    cp("dve", ynT[n % 2], psb[0][:, 256:512].rearrange("p (c t) -> p c t", c=2), writes=[("ynT", n % 2)], banks=[0])
                    dma("pool", "y%d" % (n % 2), ynT_v[n // 16][:, :, (n % 16) * 128:(n % 16 + 1) * 128], ynT[n % 2],
                        reads=[("ynT", n % 2)], writes=[("ynT_d", n // 16)])
                pending_tail.append([2, tail])

            nq = [0]

            def emit_qk_upto(lim):
                while nq[0] < min(lim, NJ):
                    j = nq[0]
                    n, a, kind, c, nj = jobs[j]
                    if kind == "sel" and j == first_sel[n]:
                        if n not in cmp_done:
                            return
                        while tkq[n]:
                            tkq[n].pop(0)()
                    qk(j)
                    nq[0] += 1

            emit_qk_upto(LA)
            for i in range(NJ):
                n, a, kind, c, nj = jobs[i]
                ex(i)
                pv(i)
                if kind == "cmp" and i == last_of[(n, "cmp")]:
                    cmp_done.add(n)
                    tkq[n] = topk_groups(n, a)
                for nn in list(tkq.keys()):
                    if tkq[nn]:
                        tkq[nn].pop(0)()
                if kind == "win" and i == last_of[(n, "win")]:
                    win_done(n, a)
                if kind == "sel" and i == last_of[(n, "sel")]:
                    while tkq[n]:
                        tkq[n].pop(0)()
                    sel_done(n, a)
                for pt in list(pending_tail):
                    pt[0] -= 1
                    if pt[0] <= 0:
                        pending_tail.remove(pt)
                        pt[1]()
                emit_qk_upto(i + 1 + LA)
                if bg is not None:
                    bgc[0] += 1
                    if bgc[0] % pace == 0:
                        next(bg, None)
            for nn in list(tkq.keys()):
                while tkq[nn]:
                    tkq[nn].pop(0)()
            for pt in list(pending_tail):
                pending_tail.remove(pt)
                pt[1]()


        def FEN_gen(s_):
            for a in range(4):
                for _ in front_end(4 * s_ + a, a, xs, xg, junk, hTn[s_ % 2], ("hTn", s_ % 2), 32 + 4 * s_ + a, gen=True):
                    yield

        def INPROJ_rest(s_):
            R = Rb[s_ % 2]
            T0 = s_ * 512
            p2 = s_ % 2
            hT = hTn[p2]
            hk = ("hTn", p2)
            HT_ALL = [(hk, a, r) for a in range(4) for r in range(2)]
            def fm_chunk(oi, ci):
                    c0, M = FMC[ci]
                    bk = 1 + (oi % 2)
                    for k in range(16):
                        mm(ps[bk][0:M, :], wn_bf[:, k, c0:c0 + M], hT[:, k, :], k == 0, k == 15, bk,
                           reads=HT_ALL + [("wn", k // 4)])
                    if ci < 4:
                        act(R[0:64, :, ci, :], ps[bk][0:64, :].rearrange("p (a q) -> p a q", a=4), AF.Identity, bias=bq8[0:64, ci:ci + 1], scale=0.125,
                            reads=["bq8"], writes=[("Rq", s_ % 2, ci)], banks=[bk])
                    elif ci in (4, 5):
                        kv = ci - 4
                        act(c2T[kv][0:64, T0 + 1:T0 + 513], ps[bk][0:64, :], AF.Identity, bias=biasFMn[0:64, ci:ci + 1],
                            reads=["biasFMn"], writes=[("c2T", kv, s_)], banks=[bk])
                        ts("dve", c2T[kv][64:128, T0:T0 + 512], ps[bk][64:128, :], biasFMn[64:128, ci:ci + 1], ALU.add,
                           reads=["biasFMn"], writes=[("c2T", kv, s_)], banks=[bk])
                    elif ci == 6:
                        ts("dve", KE[0:64, T0:T0 + 512], ps[bk][0:64, :], biasFMn[0:64, ci:ci + 1], ALU.add,
                           reads=["biasFMn"], writes=[("KE", s_)], banks=[bk])
                    else:
                        act(KWE[0:64, T0:T0 + 512], ps[bk][0:64, :], AF.Identity, bias=biasFMn[0:64, ci:ci + 1],
                            reads=["biasFMn"], writes=[("KWE", s_)], banks=[bk])

            for oi, ci in enumerate((4, 5)):
                fm_chunk(oi, ci)
            compress_st(s_)
            for oi, ci in enumerate((0, 1, 2, 3)):
                fm_chunk(oi, ci)
            for a in range(4):
                n = 4 * s_ + a
                bk = 1 + (a % 2)
                for k in range(16):
                    mm(ps[bk][:, 0:396], hT[:, k, a * 128:(a + 1) * 128], wn_bf[:, k, 640:1036], k == 0, k == 15, bk,
                       reads=[(hk, a, 0), (hk, a, 1), ("wn", k // 4)])
                tt("dve", vsaug[:, n, 0:64], ps[bk][:, 0:64], biasTMn[:, 0:64], ALU.add, reads=["biasTMn"], writes=[("vs", n)], banks=[bk])
                tt("dve", vwaug[:, n, 0:64], ps[bk][:, 64:128], biasTMn[:, 64:128], ALU.add, reads=["biasTMn"], writes=[("vw", n)], banks=[bk])
                tt("dve", tg[a % 2], ps[bk][:, 128:140], biasTMn[:, 128:140], ALU.add, reads=["biasTMn"], writes=[("tg", a % 2)], banks=[bk])
                tt("dve", tzn[a % 2], ps[bk][:, 140:396], biasTMn[:, 140:396], ALU.add, reads=["biasTMn"], writes=[("tzn", a % 2)], banks=[bk])
                act(gtn[p2][:, a, :], tg[a % 2], AF.Tanh, scale=0.5, reads=[("tg", a % 2)], writes=[("gtn", p2, a)])
                act(thzn[a % 2], tzn[a % 2], AF.Tanh, scale=0.5, reads=[("tzn", a % 2)], writes=[("thzn", a % 2)])
                stt(Zn[p2][:, a, :], thzn[a % 2], 1.0, tzn[a % 2], ALU.add, ALU.mult, reads=[("thzn", a % 2), ("tzn", a % 2)], writes=[("Zn", p2, a)])
            for oi, ci in enumerate((6, 7)):
                fm_chunk(oi, ci)

        def count_steps(g):
            return sum(1 for _ in g)

        if ntiles:
            run_interleaved(FEN_gen(0))
        for s_ in range(ntiles):
            R = Rb[s_ % 2]
            if mtiles == NST and ntiles == NST:
                if s_ == 1:
                    S.collective(lambda e: e.collective_compute("AllGather", ALU.bypass, replica_groups=GROUPS,
                                                                ins=[ymT_d[0].ap().opt()], outs=[ym_all[0].ap().opt()]),
                                 reads=[("ymT_d", 0)], writes=[("ym_all", 0)])
                elif s_ == 2:
                    S.collective(lambda e: e.collective_compute("AllGather", ALU.bypass, replica_groups=GROUPS,
                                                                ins=[ymT_d[1].ap().opt()], outs=[ym_all[1].ap().opt()]),
                                 reads=[("ymT_d", 1)], writes=[("ym_all", 1)])
                elif s_ == 4:
                    while pending_tail:
                        pending_tail.pop(0)[1]()
                    S.collective(lambda e: e.collective_compute("AllGather", ALU.bypass, replica_groups=GROUPS,
                                                                ins=[ynT_d[0].ap().opt()], outs=[yn_all[0].ap().opt()]),
                                 reads=[("ynT_d", 0)], writes=[("yn_all", 0)])
            INPROJ_rest(s_)
            compress_st2(s_)
            bg = FEN_gen(s_ + 1) if s_ + 1 < ntiles else None
            nchunks = sum((2 if 8 * n + 7 > 128 else 1) + (n - max(0, n - 4) + 1) + (n + 1) for n in range(4 * s_, 4 * s_ + 4))
            nsa_supertile(s_, R, bg, max(1, nchunks // 22))
            if bg is not None:
                for _ in bg:
                    pass
        if debug:
            for hf in range(2):
                if ntiles > 4 * hf:
                    w_ = min(2048, 512 * ntiles - 2048 * hf)
                    dma("sp", "c5", dbg["yn"][:, 2048 * hf:2048 * hf + w_], ynT_d[hf].ap()[:, 0:w_], reads=[("ynT_d", hf)], writes=["dbg_yn"])
        if upto == 2:
            return finish(["dbg_ym", "dbg_yn", "dbg_mod"])
        S.barrier()
        off[0] = PM0

        wo_bf = alloc((16, 512), BF16)
        yT = [alloc((16, 512), BF16) for _ in range(2)]
        xr = [alloc((512,), F32) for _ in range(2)]
        hres = alloc((NT, 512), F32)
        fg_bc = alloc((512,), F32)
        ssp = alloc((32,), F32)
        sq4 = alloc((4, 32), F32)
        rs_f = alloc((32,), F32)
        ot = [alloc((512,), F32) for _ in range(4)]
        junkf = alloc((512,), BF16)
        assert off[0] <= AW
        wo_v = I["wout"].rearrange("(k p) n -> p k n", p=128)
        for q4 in range(4):
            dma("pool", "w%d" % q4, wo_bf[:, 4 * q4:4 * q4 + 4, :], wo_v[:, 4 * q4:4 * q4 + 4, :], writes=[("wo", q4)])
        dma("sp", "c0", fg_bc, I["fg"].partition_broadcast(128), writes=["fg_bc"])
        S.collective(lambda e: e.collective_compute("AllGather", ALU.bypass, replica_groups=GROUPS,
                                                    ins=[ynT_d[1].ap().opt()], outs=[yn_all[1].ap().opt()]),
                     reads=[("ynT_d", 1)], writes=[("yn_all", 1)])
        yall_m = [t_.ap().rearrange("(k p) t -> p k t", p=128) for t_ in ym_all]
        yall_n = [t_.ap().rearrange("(k p) t -> p k t", p=128) for t_ in yn_all]
        for s_ in range(NST):
            yb = yT[s_ % 2]
            dma("sp", "ya%d" % (s_ % 2), yb[:, 0:8, :], yall_m[s_ // 4][:, :, (s_ % 4) * 512:(s_ % 4 + 1) * 512], reads=[("ym_all", s_ // 4)], writes=[("yT", s_ % 2, 0)])
            dma("sp", "yb%d" % (s_ % 2), yb[:, 8:16, :], yall_n[s_ // 4][:, :, (s_ % 4) * 512:(s_ % 4 + 1) * 512], reads=[("yn_all", s_ // 4)], writes=[("yT", s_ % 2, 1)])
            for a in range(4):
                n = 4 * s_ + a
                bk = 1 + (n % 2)
                dma("sp", "x%d" % (n % 2), xr[n % 2], I["xcol"][n * 128:(n + 1) * 128, :], writes=[("xr", n % 2)])
                for k in range(16):
                    mm(ps[bk][:, :], yb[:, k, a * 128:(a + 1) * 128], wo_bf[:, k, :], k == 0, k == 15, bk,
                       reads=[("yT", s_ % 2, k // 8), ("wo", k // 4)])
                tt("dve", hres[:, n, :], ps[bk][:, :], gate_bc, ALU.mult, reads=["gate_bc"], writes=[("hres", n)], banks=[bk])
                tt("dve", hres[:, n, :], hres[:, n, :], xr[n % 2], ALU.add, reads=[("hres", n), ("xr", n % 2)], writes=[("hres", n)])
                act(junkf, hres[:, n, :], AF.Square, accum=ssp[:, n:n + 1], reads=[("hres", n)], writes=[("ssp", n)])
                tt("pool", hres[:, n, :], hres[:, n, :], fg_bc, ALU.mult, reads=[("hres", n), ("ssp", n), "fg_bc"], writes=[("hres", n)])
        dma("sp", "c1", sq_in.ap(), ssp, reads=[("ssp", n) for n in range(NT)], writes=["sq_in"])
        S.collective(lambda e: e.collective_compute("AllGather", ALU.bypass, replica_groups=GROUPS,
                                                    ins=[sq_in.ap().opt()], outs=[sq_all.ap().opt()]),
                     reads=["sq_in"], writes=["sq_all"])
        dma("sp", "c2", sq4, sq_all.ap().rearrange("(r p) n -> p r n", p=128), reads=["sq_all"], writes=["sq4"])
        tt("dve", rs_f, sq4[:, 0, :], sq4[:, 1, :], ALU.add, reads=["sq4"], writes=["rs_f"])
        tt("dve", rs_f, rs_f, sq4[:, 2, :], ALU.add, reads=["sq4", "rs_f"], writes=["rs_f"])
        tt("dve", rs_f, rs_f, sq4[:, 3, :], ALU.add, reads=["sq4", "rs_f"], writes=["rs_f"])
        ts("dve", rs_f, rs_f, 1.0 / D, ALU.mult, EPS, ALU.add, reads=["rs_f"], writes=["rs_f"])
        tt("pool", rs_f, rs_f, neghalf, ALU.pow, reads=["rs_f", "neghalf"], writes=["rs_f"])
        for n in range(NT):
            eng3 = ("dve", "act")[n % 2]
            if eng3 == "act":
                act(ot[n % 4], hres[:, n, :], AF.Copy, scale=rs_f[:, n:n + 1], reads=[("hres", n), "rs_f"], writes=[("ot", n % 4)])
            else:
                ts(eng3, ot[n % 4], hres[:, n, :], rs_f[:, n:n + 1], ALU.mult, reads=[("hres", n), "rs_f"], writes=[("ot", n % 4)])
            dma("sp", "o%d" % (n % 4), out_d[n * 128:(n + 1) * 128, :], ot[n % 4], reads=[("ot", n % 4)], writes=[("out", n)])
        return finish([("out", n) for n in range(NT)] + (["dbg_ym", "dbg_yn", "dbg_mod"] if debug else []))


def _consts(j):
    slopes = np.array([2.0 ** (-8.0 * (4 * j + hl + 1) / 16) for hl in range(4)], np.float64)
    p = np.arange(128)
    expand = np.zeros((64, T), np.float32)
    expand[np.arange(T) // 64, np.arange(T)] = 1.0
    e0 = np.zeros((64, T), np.float32)
    e0[0, :] = 1.0
    rowt = np.zeros((64, 4, 128), np.float32)
    for hl in range(4):
        rowt[:, hl, :] = (-slopes[hl] * np.arange(128))[None, :]
    SHIFT = 30.0
    abt = np.zeros((128, 4, 32), np.float64)
    for hl in range(4):
        for rel in range(32):
            abt[:, hl, rel] = slopes[hl] * (p - 128.0 * rel) - SHIFT
    cbt = np.zeros((128, 4, 2, 32), np.float64)
    for hl in range(4):
        for jc in range(2):
            for n in range(32):
                cbt[:, hl, jc, n] = slopes[hl] * (16.0 * (128 * jc + p) + 15.5 - 128.0 * n) - SHIFT
    qq = np.arange(128)
    dgt = np.zeros((128, 4, 128), np.float64)
    fet = np.zeros((128, 4, 128), np.float64)
    for hl in range(4):
        dgt[:, hl, :] = (slopes[hl] * p - SHIFT)[:, None] + np.where(p[:, None] > qq[None, :], -30000.0, 0.0)
        fet[:, hl, :] = (slopes[hl] * (p - 512.0) - SHIFT)[:, None] + np.where(p[:, None] <= qq[None, :], -30000.0, 0.0)
    ovl1 = np.zeros((128, 2, 64), np.float32)
    for jc in range(2):
        for pp in range(128):
            jj = 128 * jc + pp
            ovl1[pp, jc, 0] = 1.0
            for s in range(63):
                if 16 * jj <= 64 * s + 63 and 16 * jj + 31 >= 64 * s:
                    ovl1[pp, jc, 1 + s] = 1.0
    return dict(
        identb=np.eye(128, dtype=np.float32).astype(bf), identf=np.eye(128, dtype=np.float32),
        triu=np.triu(np.ones((128, 128), np.float32)), onesf=np.ones((128, 128), np.float32),
        expand=expand.astype(bf), e0=e0.astype(bf), rowt=rowt.reshape(64, 512).astype(np.float32),
        abt=abt.reshape(128, 128).astype(np.float32), cbt=cbt.reshape(128, 256).astype(np.float32),
        ovl1=ovl1.reshape(128, 128).astype(bf), dgt=dgt.reshape(128, 512).astype(np.float32),
        fet=fet.reshape(128, 512).astype(np.float32))


def make_in_maps(x, c, ln_g, w_ada, b_ada, w_in, b_in, m_conv_w, m_conv_b, m_wq, m_wk, m_norm_w,
                 m_skip, m_f_bias, n_pos_k, n_pos_v, n_w1_k, n_w2_k, n_w1_v, n_w2_v, w_out, final_g):
    f = lambda v: np.ascontiguousarray(np.asarray(v, dtype=np.float32))
    x, c, ln_g, w_ada, b_ada, w_in, b_in = map(f, (x, c, ln_g, w_ada, b_ada, w_in, b_in))
    m_conv_w, m_conv_b, m_wq, m_wk, m_norm_w, m_skip, m_f_bias = map(f, (m_conv_w, m_conv_b, m_wq, m_wk, m_norm_w, m_skip, m_f_bias))
    n_pos_k, n_pos_v, n_w1_k, n_w2_k, n_w1_v, n_w2_v, w_out, final_g = map(f, (n_pos_k, n_pos_v, n_w1_k, n_w2_k, n_w1_v, n_w2_v, w_out, final_g))
    W = w_in[0]
    B = b_in[0]
    O = dict(mx=0, mv=1024, mo=2048, mz=3072, mi=4096, mf=4100, nq=4104, nkc=5128, nvc=5384, nks=5640,
             nvs=5896, nkw=6152, nvw=6408, ng=6664, nz=6712)
    maps = []
    for r in range(8):
        b, j = divmod(r, 4)
        r256 = np.arange(256 * j, 256 * j + 256)
        r64 = np.arange(64 * j, 64 * j + 64)
        cm = np.concatenate([O["mx"] + r256, O["mv"] + r256, O["mo"] + r256, O["mz"] + r256, [O["mi"] + j], [O["mf"] + j]])
        cn_fm = np.concatenate([O["nq"] + r256, O["nkc"] + r64, O["nkc"] + r64, O["nvc"] + r64, O["nvc"] + r64,
                                O["nks"] + r64, O["nkw"] + r64])
        cn_tm = np.concatenate([O["nvs"] + r64, O["nvw"] + r64, O["ng"] + np.arange(12 * j, 12 * j + 12), O["nz"] + r256])
        cn = np.concatenate([cn_fm, cn_tm])
        bn_fm = np.zeros((128, 8), np.float32)
        for h in range(4):
            bn_fm[0:64, h] = B[O["nq"] + 256 * j + 64 * h + np.arange(64)]
        bn_fm[0:64, 4] = B[O["nkc"] + r64]; bn_fm[64:128, 4] = B[O["nkc"] + r64]
        bn_fm[0:64, 5] = B[O["nvc"] + r64]; bn_fm[64:128, 5] = B[O["nvc"] + r64]
        bn_fm[0:64, 6] = B[O["nks"] + r64]
        bn_fm[0:64, 7] = B[O["nkw"] + r64]
        wrows = np.concatenate([np.concatenate([np.arange(256 * jj, 256 * jj + 256) for jj in range(4)]),
                                np.concatenate([1024 + np.arange(256 * jj, 256 * jj + 256) for jj in range(4)])])
        pos2 = lambda pz: np.ascontiguousarray(pz.reshape(16, 2, 64).transpose(1, 2, 0).reshape(128, 16))
        m = dict(
            x=x[b], xcol=np.ascontiguousarray(x[b][:, 512 * j:512 * j + 512]),
            c_col=np.ascontiguousarray(c[b].reshape(16, 128).T),
            wada=np.ascontiguousarray(np.concatenate([w_ada[0][:, 1024 * j:1024 * j + 1024],
                                                      w_ada[0][:, 4096 + 512 * j:4096 + 512 * j + 512]], axis=1)),
            bada=np.concatenate([b_ada[0][1024 * j:1024 * j + 1024], b_ada[0][4096 + 512 * j:4096 + 512 * j + 512]])[None, :],
            lng=ln_g[0][None, :],
            wm=np.ascontiguousarray(W[:, cm]),
            bm_fm=np.ascontiguousarray(B[O["mx"] + r256].reshape(2, 128).T),
            bm_tm=B[cm[256:]][None, :],
            wn=np.ascontiguousarray(W[:, cn]), bn_fm=bn_fm, bn_tm=B[cn_tm][None, :],
            convw=np.ascontiguousarray(m_conv_w[0][:, r256].reshape(4, 2, 128).transpose(2, 1, 0).reshape(128, 8)),
            convb=np.ascontiguousarray(m_conv_b[0][r256].reshape(2, 128).T),
            wq=m_wq[0, j], wk=m_wk[0, j],
            normw=m_norm_w[0][r256][None, :], skip=m_skip[0][r256][None, :],
            fbias=m_f_bias[0][j:j + 1][None, :],
            posk2=pos2(n_pos_k[0]), posv2=pos2(n_pos_v[0]),
            w1k=n_w1_k[0], w1v=n_w1_v[0], w2k=n_w2_k[0], w2v=n_w2_v[0],
            wout=np.ascontiguousarray(w_out[0][wrows][:, 512 * j:512 * j + 512]),
            fg=final_g[512 * j:512 * j + 512][None, :],
        )
        m.update(_consts(j))
        m = {k: np.ascontiguousarray(v) for k, v in m.items()}
        maps.append(m)
    return maps


_NC_CACHE = {}


def kernel(**inputs):
    maps = make_in_maps(**inputs)
    if "nc" not in _NC_CACHE:
        _NC_CACHE["nc"] = build(False)
    nc = _NC_CACHE["nc"]
    res = run_bass_kernel_spmd(nc, maps, core_ids=list(range(8)))
    out = np.zeros((2, T, D), np.float32)
    for r in range(8):
        b, j = divmod(r, 4)
        out[b, :, 512 * j:512 * j + 512] = res.results[r]["out"]
    return out
```
